# Optimizing a Trainium2 kernel written in Bass

```python
import jax, jax.numpy as jnp
from jax import lax
import numpy as np

D_MODEL = 2048
BATCH = 1
SEQ = 16384
DEPTH = 4
DEC_BATCH = 2
DEC_SEQ = 8192
PAST_LEN = 128

N_MIXERS = 2
N_CONV = (DEPTH + 1) // 2
N_ATTN = DEPTH // 2
N_HEADS = 16
HEAD_DIM = D_MODEL // N_HEADS
N_KV_HEADS = 4
GROUP = N_HEADS // N_KV_HEADS
ROT_DIM = HEAD_DIM // 4
ROPE_THETA = 500000.0
WINDOW = 128
BLOCK = 128
CONV_WIDTH = 31
D_FF = -(-8 * D_MODEL // (3 * 256)) * 256
EPS = 1e-6
NEG = -1e30

kernel_name = "hybrid_conformer_swa_encoder"


def _rmsnorm(x, g):
    xf = x.astype(jnp.float32)
    y = xf * lax.rsqrt(jnp.mean(xf * xf, axis=-1, keepdims=True) + EPS)
    return (y * g.astype(jnp.float32)).astype(x.dtype)


def _layernorm(x, g, b):
    xf = x.astype(jnp.float32)
    mu = jnp.mean(xf, axis=-1, keepdims=True)
    xc = xf - mu
    y = xc * lax.rsqrt(jnp.mean(xc * xc, axis=-1, keepdims=True) + EPS)
    return (y * g.astype(jnp.float32) + b.astype(jnp.float32)).astype(x.dtype)


def _partial_rope(x, pos):
    inv_freq = ROPE_THETA ** (-jnp.arange(0, ROT_DIM, 2, dtype=jnp.float32) / ROT_DIM)
    ang = pos[:, None] * inv_freq[None, :]
    cos = jnp.concatenate([jnp.cos(ang), jnp.cos(ang)], -1)[None, :, None, :]
    sin = jnp.concatenate([jnp.sin(ang), jnp.sin(ang)], -1)[None, :, None, :]
    xr = x[..., :ROT_DIM].astype(jnp.float32)
    x1, x2 = jnp.split(xr, 2, axis=-1)
    rot = jnp.concatenate([-x2, x1], axis=-1)
    xr = (xr * cos + rot * sin).astype(x.dtype)
    return jnp.concatenate([xr, x[..., ROT_DIM:]], axis=-1)


def _conv_module(h, w_pw1, w_dw, b_dw, ln_g, ln_b, w_pw2):
    a, g = jnp.split(h @ w_pw1, 2, axis=-1)
    u = a * jax.nn.sigmoid(g)
    u = lax.conv_general_dilated(
        u, w_dw[:, None, :], window_strides=(1,),
        padding=[(CONV_WIDTH // 2, CONV_WIDTH // 2)],
        dimension_numbers=('NWC', 'WIO', 'NWC'),
        feature_group_count=D_MODEL) + b_dw
    u = jax.nn.silu(_layernorm(u, ln_g, ln_b))
    return u @ w_pw2


def _band(t, nb):
    b = t.shape[0]
    tp = jnp.pad(t, ((0, 0), (BLOCK, BLOCK), (0, 0), (0, 0)))
    tp = tp.reshape(b, nb + 2, BLOCK, N_KV_HEADS, HEAD_DIM)
    return jnp.concatenate([tp[:, :-2], tp[:, 1:-1], tp[:, 2:]], axis=2)


def _window_attention(h, w_q, w_k, w_v, w_o, sink):
    b, s, _ = h.shape
    nb = s // BLOCK
    pos = jnp.arange(s, dtype=jnp.float32)
    q = _partial_rope((h @ w_q).reshape(b, s, N_HEADS, HEAD_DIM), pos)
    k = _partial_rope((h @ w_k).reshape(b, s, N_KV_HEADS, HEAD_DIM), pos)
    v = (h @ w_v).reshape(b, s, N_KV_HEADS, HEAD_DIM)
    qb = q.reshape(b, nb, BLOCK, N_KV_HEADS, GROUP, HEAD_DIM)
    kb, vb = _band(k, nb), _band(v, nb)
    sc = jnp.einsum('bnqkgd,bnjkd->bnkgqj', qb, kb,
                    preferred_element_type=jnp.float32) * (HEAD_DIM ** -0.5)
    blk = jnp.arange(nb)[:, None, None]
    q_pos = blk * BLOCK + jnp.arange(BLOCK)[None, :, None]
    k_pos = blk * BLOCK - BLOCK + jnp.arange(3 * BLOCK)[None, None, :]
    valid = (jnp.abs(q_pos - k_pos) <= WINDOW) & (k_pos >= 0) & (k_pos < s)
    sc = jnp.where(valid[None, :, None, None, :, :], sc, NEG)
    sink_l = sink.astype(jnp.float32).reshape(1, 1, N_KV_HEADS, GROUP, 1, 1)
    m = jnp.maximum(jnp.max(sc, axis=-1, keepdims=True), sink_l)
    p = jnp.exp(sc - m)
    p = p / (jnp.sum(p, axis=-1, keepdims=True) + jnp.exp(sink_l - m))
    o = jnp.einsum('bnkgqj,bnjkd->bnqkgd', p.astype(vb.dtype), vb)
    return o.reshape(b, s, D_MODEL) @ w_o


def _swiglu(h, w_gate, w_up, w_down):
    return (jax.nn.silu(h @ w_gate) * (h @ w_up)) @ w_down


def _trunk(x, c, w_ada, b_ada, norm_g, w_gate, w_up, w_down,
           conv_w_pw1, conv_w_dw, conv_b_dw, conv_ln_g, conv_ln_b, conv_w_pw2,
           attn_w_q, attn_w_k, attn_w_v, attn_w_o, attn_sink):
    c_act = jax.nn.silu(c)
    for i in range(DEPTH):
        mod = c_act @ w_ada[i] + b_ada[i]
        sh_m, sc_m, gt_m, sh_f, sc_f, gt_f = [t[:, None, :] for t in jnp.split(mod, 6, axis=-1)]
        h = _rmsnorm(x, norm_g[i, 0]) * (1.0 + sc_m) + sh_m
        j = i // N_MIXERS
        if i % N_MIXERS == 0:
            out = _conv_module(h, conv_w_pw1[j], conv_w_dw[j], conv_b_dw[j],
                               conv_ln_g[j], conv_ln_b[j], conv_w_pw2[j])
        else:
            out = _window_attention(h, attn_w_q[j], attn_w_k[j], attn_w_v[j],
                                    attn_w_o[j], attn_sink[j])
        x = x + gt_m * _rmsnorm(out, norm_g[i, 1])
        h = _rmsnorm(x, norm_g[i, 2]) * (1.0 + sc_f) + sh_f
        x = x + gt_f * _rmsnorm(_swiglu(h, w_gate[i], w_up[i], w_down[i]), norm_g[i, 3])
    return x


def setup_inputs(seed: int = 0) -> dict:
    key = jax.random.key(seed)
    ks = jax.random.split(key, 24)
    f32 = jnp.float32
    nrm = lambda k, shape, scale: jax.random.normal(k, shape, f32) * scale
    D = D_MODEL
    return {
        "x_prompt": nrm(ks[0], (BATCH, SEQ, D), 1.0),
        "x_sample": nrm(ks[1], (DEC_BATCH, DEC_SEQ, D), 1.0),
        "c_prompt": nrm(ks[2], (BATCH, D), 1.0),
        "c_sample": nrm(ks[3], (DEC_BATCH, D), 1.0),
        "w_ada": nrm(ks[4], (DEPTH, D, 6 * D), D ** -0.5),
        "b_ada": nrm(ks[5], (DEPTH, 6 * D), 0.01),
        "norm_g": 1.0 + nrm(ks[6], (DEPTH, 4, D), 0.05),
        "w_gate": nrm(ks[7], (DEPTH, D, D_FF), D ** -0.5),
        "w_up": nrm(ks[8], (DEPTH, D, D_FF), D ** -0.5),
        "w_down": nrm(ks[9], (DEPTH, D_FF, D), D_FF ** -0.5),
        "conv_w_pw1": nrm(ks[10], (N_CONV, D, 2 * D), D ** -0.5),
        "conv_w_dw": nrm(ks[11], (N_CONV, CONV_WIDTH, D), CONV_WIDTH ** -0.5),
        "conv_b_dw": nrm(ks[12], (N_CONV, D), 0.01),
        "conv_ln_g": 1.0 + nrm(ks[13], (N_CONV, D), 0.05),
        "conv_ln_b": nrm(ks[14], (N_CONV, D), 0.01),
        "conv_w_pw2": nrm(ks[15], (N_CONV, D, D), D ** -0.5),
        "attn_w_q": nrm(ks[16], (N_ATTN, D, N_HEADS * HEAD_DIM), D ** -0.5),
        "attn_w_k": nrm(ks[17], (N_ATTN, D, N_KV_HEADS * HEAD_DIM), D ** -0.5),
        "attn_w_v": nrm(ks[18], (N_ATTN, D, N_KV_HEADS * HEAD_DIM), D ** -0.5),
        "attn_w_o": nrm(ks[19], (N_ATTN, N_HEADS * HEAD_DIM, D), (N_HEADS * HEAD_DIM) ** -0.5),
        "attn_sink": nrm(ks[20], (N_ATTN, N_HEADS), 0.5),
    }


def reference(x_prompt, x_sample, c_prompt, c_sample, w_ada, b_ada, norm_g,
              w_gate, w_up, w_down, conv_w_pw1, conv_w_dw, conv_b_dw, conv_ln_g,
              conv_ln_b, conv_w_pw2, attn_w_q, attn_w_k, attn_w_v, attn_w_o, attn_sink):
    y_prompt = _trunk(x_prompt, c_prompt, w_ada, b_ada, norm_g, w_gate, w_up, w_down,
                      conv_w_pw1, conv_w_dw, conv_b_dw, conv_ln_g, conv_ln_b, conv_w_pw2,
                      attn_w_q, attn_w_k, attn_w_v, attn_w_o, attn_sink)
    y_sample = _trunk(x_sample, c_sample, w_ada, b_ada, norm_g, w_gate, w_up, w_down,
                      conv_w_pw1, conv_w_dw, conv_b_dw, conv_ln_g, conv_ln_b, conv_w_pw2,
                      attn_w_q, attn_w_k, attn_w_v, attn_w_o, attn_sink)
    return (y_prompt, y_sample)
```

```python
import os
import numpy as np
import ml_dtypes
import concourse.bass as bass
import concourse.mybir as mybir
from concourse.bass_utils import run_bass_kernel_spmd

F32 = mybir.dt.float32
DBG_SKIP = os.environ.get('DBG_SKIP', '')
BF16 = mybir.dt.bfloat16
AF = mybir.ActivationFunctionType
ALU = mybir.AluOpType
AX = mybir.AxisListType

D = 2048
NCH = 16
DFF = 5632
FCH = 44
NHEADS = 16
NKV = 4
CW = 31
EPS = 1e-6
ROT = 32
THETA = 500000.0
NCORES = 8
TOK_CORE = 4096
BASEB = 4
NBLK = 32 + 2 * BASEB
NTOK = NBLK * 128
SUBB = 3
NSUB = 2
NH = 2
WCAP = 8192
NWB = 3
ARENA_BYTES = 146 * 1024
SCALE = 128 ** -0.5
NEGM = -30000.0

PV_C = 0
PV_L = 16
PV_LSZ = 160
PV_CONV = PV_L + 4 * PV_LSZ
PV_CSZ = 544
PV_ATT = PV_CONV + 2 * PV_CSZ
PV_ASZ = 16
NPV = PV_ATT + 2 * PV_ASZ


class Buf:
    __slots__ = ("name", "w", "r", "multi", "ws", "excl")

    def __init__(self, name, multi=False, excl=False):
        self.name = name
        self.w = None
        self.r = {}
        self.multi = multi
        self.ws = {}
        self.excl = excl

    def wevents(self):
        if self.multi:
            return list(self.ws.items())
        return [self.w] if self.w is not None else []

    def setw(self, ev):
        if self.multi:
            self.ws[ev[0]] = max(self.ws.get(ev[0], 0), ev[1])
        else:
            self.w = ev
            self.r = {}


class Trk:
    def __init__(self, nc):
        self.nc = nc
        self.dry = False
        self.eng = {"pe": nc.tensor, "act": nc.scalar, "dve": nc.vector, "pool": nc.gpsimd, "sp": nc.sync}
        self.sems = {}
        self.cnt = {}
        self.seen = {k: {} for k in self.eng}
        self.ninstr = 0

    def semh(self, key):
        h = self.sems.get(key)
        if h is None:
            h = self.nc.alloc_semaphore("s_" + key)
            self.sems[key] = h
            self.cnt[key] = 0
        return h

    def _wait(self, e, deps):
        seen = self.seen[e]
        for (k, v) in deps:
            if k == e and e == "pe":
                continue
            if seen.get(k, 0) < v:
                self.eng[e].wait_ge(self.semh(k), v)
                seen[k] = v

    def op(self, e, fn, reads=(), writes=(), signal=True):
        if self.dry:
            return
        deps = []
        for b in reads:
            deps += b.wevents()
            if b.excl:
                deps += [(k, v) for (k, v) in b.r.items() if k != e]
        for b in writes:
            deps += b.wevents()
            deps += list(b.r.items())
        self._wait(e, deps)
        ins = fn()
        self.ninstr += 1
        self.semh(e)
        if signal:
            self.cnt[e] += 1
            ins.then_inc(self.sems[e], 1)
            ev = (e, self.cnt[e])
        else:
            ev = (e, self.cnt[e] + 1)
        for b in reads:
            if b.r.get(ev[0], 0) < ev[1]:
                b.r[ev[0]] = ev[1]
        for b in writes:
            b.setw(ev)

    def dma(self, q, pairs, reads, writes, sembuf):
        if self.dry:
            return
        key = "d_" + sembuf.name
        self.semh(key)
        deps = []
        for b in reads:
            deps += b.wevents()
        for b in writes:
            if not b.multi:
                deps += b.wevents()
            deps += list(b.r.items())
        if self.cnt[key]:
            deps.append((key, self.cnt[key]))
        self._wait(q, deps)
        for (o, i) in pairs:
            self.eng[q].dma_start(out=o, in_=i).then_inc(self.sems[key], 16)
            self.cnt[key] += 16
            self.ninstr += 1
        ev = (key, self.cnt[key])
        for b in reads:
            if b.r.get(key, 0) < ev[1]:
                b.r[key] = ev[1]
        for b in writes:
            b.setw(ev)

    def barrier(self, engines=("pe", "act", "dve", "pool", "sp")):
        if self.dry:
            return
        deps = [(k, v) for k, v in self.cnt.items() if v > 0]
        for e in engines:
            self._wait(e, deps)


def split_blocks(nb, unit):
    out = []
    while nb > 0:
        t = min(unit, nb)
        out.append(t)
        nb -= t
    return out


def supertiles(b0, b1):
    res = []
    b = b0
    while b < b1:
        nb = min(SUBB * NSUB, b1 - b)
        if nb == SUBB * NSUB:
            subs = [SUBB] * NSUB
        else:
            h = (nb + 1) // 2
            subs = [h, nb - h] if nb - h > 0 else [h]
        st = []
        bb = b
        for s in subs:
            st.append((bb * 128, s * 128))
            bb += s
        res.append(st)
        b += nb
    return res


def subtiles(b0, b1):
    return [s for st in supertiles(b0, b1) for s in st]


class Prog:
    def __init__(self, depth, dbg=None):
        self.depth = depth
        self.dbg = dbg
        nc = bass.Bass("TRN2", target_bir_lowering=False)
        self.nc = nc
        self.t = Trk(nc)
        dt = nc.dram_tensor
        self.x_in = dt("x_ext", [NTOK, D], F32, kind="ExternalInput").ap()
        self.pvec_in = dt("pvec", [128, NPV], F32, kind="ExternalInput").ap()
        self.tokmask_in = dt("tokmask", [128, NTOK], F32, kind="ExternalInput").ap()
        self.ropec_in = dt("ropec", [128, NTOK], F32, kind="ExternalInput").ap()
        self.ropes_in = dt("ropes", [128, NTOK], F32, kind="ExternalInput").ap()
        self.amask_in = dt("amask", [NBLK, 128, 384], F32, kind="ExternalInput").ap()
        self.cmat_in = dt("cmat", [2, 128, 128], F32, kind="ExternalInput").ap()
        if not dbg:
            self.w_ada = dt("w_ada", [4, D, 6 * D], F32, kind="ExternalInput").ap()
            self.w_gate = dt("w_gate", [4, D, DFF], F32, kind="ExternalInput").ap()
            self.w_up = dt("w_up", [4, D, DFF], F32, kind="ExternalInput").ap()
            self.w_down = dt("w_down", [4, DFF, D], F32, kind="ExternalInput").ap()
            self.w_pw1 = dt("conv_w_pw1", [2, D, 2 * D], F32, kind="ExternalInput").ap()
            self.w_pw2 = dt("conv_w_pw2", [2, D, D], F32, kind="ExternalInput").ap()
        self.w_q = dt("attn_w_q", [2, D, D], F32, kind="ExternalInput").ap()
        self.w_k = dt("attn_w_k", [2, D, 512], F32, kind="ExternalInput").ap()
        self.w_v = dt("attn_w_v", [2, D, 512], F32, kind="ExternalInput").ap()
        self.w_o = dt("attn_w_o", [2, D, D], F32, kind="ExternalInput").ap()
        self.y_out = dt("y", [TOK_CORE, D], F32, kind="ExternalOutput").ap()
        self.xs = dt("xs", [NCH, 128, NTOK], F32, kind=("ExternalOutput" if dbg else "Internal")).ap()
        self.us = dt("us", [NCH, 128, NTOK], F32, kind="Internal").ap()
        self.qs = dt("qs", [NCH, 128, NTOK], BF16, kind="Internal").ap()
        self.ks = dt("ks", [NKV, 128, NTOK], BF16, kind="Internal").ap()
        self.vs = dt("vs", [NTOK, 512], BF16, kind="Internal").ap()
        self.b_xs = Buf("xs", multi=True)
        self.b_us = Buf("us", multi=True)
        self.b_qs = Buf("qs", multi=True)
        self.b_ks = Buf("ks", multi=True)
        self.b_vs = Buf("vs", multi=True)
        self.b_in = Buf("inputs")
        al = nc.alloc_sbuf_tensor
        self.pvec = al("pvec_sb", [128, NPV], F32)
        self.b_pvec = Buf("pvec")
        self.identf = al("identf", [128, 128], F32)
        self.identb = al("identb", [128, 128], BF16)
        self.rotb = al("rotb", [128, 128], BF16)
        self.onesb = al("onesb", [128, 128], BF16)
        self.b_const = Buf("const", multi=True)
        self.cact = al("cact", [128, NCH], BF16)
        self.b_cact = Buf("cact")
        self.mod = al("mod", [128, 4, 96], F32)
        self.prm = al("prm", [128, 4, 4, NCH], F32)
        self.negsink = al("negsink", [128, 2, NHEADS], F32)
        self.b_mod = Buf("mod")
        self.wring = [al("wring%d" % i, [128, WCAP], BF16) for i in range(NWB)]
        self.b_wring = [Buf("wring%d" % i) for i in range(NWB)]
        self.arena = al("arena", [128, ARENA_BYTES // 2], BF16)
        self.arena_off = 0
        print("sbuf remaining after alloc", nc.sbuf_bytes_remaining)
        self.psum = [nc.alloc_psum_tensor("ps%d" % i, [128, 512], F32) for i in range(8)]
        self.b_psum = [Buf("ps%d" % i, excl=True) for i in range(8)]
        self.ps_i = 0
        self.nbuf = 0
        self.fills = []
        self.fill_i = 0
        self.fill_issued = 0

    def newbuf(self, name):
        self.nbuf += 1
        return Buf("%s_%d" % (name, self.nbuf))

    def arena_reset(self):
        self.arena_off = 0

    def carve(self, nbytes, dtype, shape):
        nbytes = (nbytes + 63) // 64 * 64
        off = self.arena_off
        assert off + nbytes <= ARENA_BYTES, ("arena overflow", off, nbytes)
        self.arena_off += nbytes
        ap = self.arena[:, off // 2:(off + nbytes) // 2]
        n = int(np.prod(shape))
        if dtype == F32:
            ap = ap.bitcast(F32)[:, 0:n]
        else:
            ap = ap[:, 0:n]
        if len(shape) == 2:
            ap = ap.rearrange("p (a b) -> p a b", a=shape[0])
        elif len(shape) == 3:
            ap = ap.rearrange("p (a b c) -> p a b c", a=shape[0], b=shape[1])
        return ap

    def cf32(self, shape):
        return self.carve(int(np.prod(shape)) * 4, F32, shape)

    def cbf(self, shape):
        return self.carve(int(np.prod(shape)) * 2, BF16, shape)

    def ps_next(self):
        i = self.ps_i
        self.ps_i = (i + 1) % 8
        return self.psum[i], self.b_psum[i]

    def pv(self, off, n):
        return self.pvec[:, off:off + n]

    def wfill_plan(self, parts):
        self.fills.append(parts)
        return len(self.fills) - 1

    def wissue(self):
        i = self.fill_issued
        if i >= len(self.fills):
            return
        self.fill_issued += 1
        slot = i % NWB
        wb = self.wring[slot]
        pairs = []
        for (off, kdim, ncols, src) in self.fills[i]:
            dst = wb[:, off:off + kdim * ncols].rearrange("p (k n) -> p k n", k=kdim)
            pairs.append((dst, src))
        self.t.dma("pool", pairs, reads=[self.b_in], writes=[self.b_wring[slot]], sembuf=self.b_wring[slot])

    def wnext(self, parts):
        if self.t.dry:
            self.wfill_plan(parts)
            return self.wring[0], self.b_wring[0]
        i = self.fill_i
        self.fill_i += 1
        assert i < self.fill_issued or True
        while self.fill_issued <= i:
            self.wissue()
        slot = i % NWB
        return self.wring[slot], self.b_wring[slot]

    def wrelease(self):
        if self.t.dry:
            return
        while self.fill_issued < min(len(self.fills), self.fill_i + NWB - 0):
            if self.fill_issued - NWB >= self.fill_i:
                break
            self.wissue()

    def rstd_from_ps(self, ss_ps, b_ss, n, rstd, b_rstd, tmp, b_tmp):
        t = self.t
        nc = self.nc
        t.op("dve", lambda: nc.vector.tensor_scalar(out=tmp[:, 0:n], in0=ss_ps[:, 0:n], scalar1=1.0 / D, scalar2=EPS,
                                                    op0=ALU.mult, op1=ALU.add), reads=[b_ss], writes=[b_tmp])
        t.op("dve", lambda: nc.vector.reciprocal(out=tmp[:, 0:n], in_=tmp[:, 0:n]), reads=[b_tmp], writes=[b_tmp])
        t.op("act", lambda: nc.scalar.activation(out=rstd[:, 0:n], in_=tmp[:, 0:n], func=AF.Sqrt), reads=[b_tmp], writes=[b_rstd])

    def sumsq(self, src, b_src, n, sq, b_sq, from_psum=False):
        t = self.t
        nc = self.nc
        ps, b_ps = self.ps_next()
        for k in range(NCH):
            j = k % 4
            t.op("act", lambda k=k, j=j: nc.scalar.activation(out=sq[:, j, 0:n], in_=src[:, k, 0:n], func=AF.Square),
                 reads=[b_src], writes=[b_sq[j]])
            t.op("pe", lambda k=k, j=j: nc.tensor.matmul(ps[:, 0:n], lhsT=self.onesb[:], rhs=sq[:, j, 0:n],
                                                        start=(k == 0), stop=(k == NCH - 1)),
                 reads=[b_sq[j], self.b_const], writes=([b_ps] if k == 0 else []), signal=True)
        b_ps.setw(("pe", t.cnt.get("pe", 0))) if not t.dry else None
        return ps, b_ps

    def prenorm(self, x, b_x, n, a_ap, sh_ap, h, b_h, scr):
        t = self.t
        nc = self.nc
        ps, b_ps = self.sumsq(x, b_x, n, scr["sq"], scr["b_sq"])
        self.rstd_from_ps(ps, b_ps, n, scr["rstd"], scr["b_rstd"], scr["tmp"], scr["b_tmp"])
        rstd = scr["rstd"]
        for k in range(NCH):
            j = k % 2
            tk = scr["tk"][:, j, 0:n]
            t.op("dve", lambda k=k, tk=tk: nc.vector.scalar_tensor_tensor(out=tk, in0=x[:, k, 0:n], scalar=a_ap[:, k:k + 1],
                                                                          in1=rstd[:, 0:n], op0=ALU.mult, op1=ALU.mult),
                 reads=[b_x, scr["b_rstd"], self.b_mod], writes=[scr["b_tk"][j]])
            t.op("pool", lambda k=k, tk=tk: nc.gpsimd.tensor_scalar(out=h[:, k, 0:n], in0=tk, scalar1=sh_ap[:, k:k + 1], scalar2=None,
                                                                    op0=ALU.add),
                 reads=[scr["b_tk"][j], self.b_mod], writes=[b_h])

    def norm_scratch(self, nmax):
        s = {}
        s["sq"] = self.cbf([4, nmax])
        s["b_sq"] = [self.newbuf("sq") for _ in range(4)]
        s["rstd"] = self.cf32([nmax])[:, 0:nmax] if False else self.carve(nmax * 4, F32, [nmax])
        s["b_rstd"] = self.newbuf("rstd")
        s["tmp"] = self.carve(nmax * 4, F32, [nmax])
        s["b_tmp"] = self.newbuf("tmp")
        s["tk"] = self.cf32([2, nmax])
        s["b_tk"] = [self.newbuf("tk") for _ in range(2)]
        return s

    def load_x(self, x, b_x, tok0, n):
        self.t.dma("sp", [(x[:, :, 0:n], self.xs[:, :, tok0:tok0 + n].rearrange("c p t -> p c t"))],
                   reads=[self.b_xs], writes=[b_x], sembuf=b_x)

    def store_x(self, x, b_x, tok0, n):
        self.t.dma("sp", [(self.xs[:, :, tok0:tok0 + n].rearrange("c p t -> p c t"), x[:, :, 0:n])],
                   reads=[b_x], writes=[self.b_xs], sembuf=b_x)

    def phase_init(self):
        t = self.t
        nc = self.nc
        t.dma("sp", [(self.pvec[:], self.pvec_in)], reads=[self.b_in], writes=[self.b_pvec], sembuf=self.b_pvec)
        t.dma("sp", [(self.identf[:], self.cmat_in[0])], reads=[self.b_in], writes=[self.b_const], sembuf=self.b_const)
        t.dma("pool", [(self.identb[:], self.cmat_in[0]), (self.rotb[:], self.cmat_in[1])], reads=[self.b_in],
              writes=[self.b_const], sembuf=self.newbuf("constb"))
        t.op("dve", lambda: nc.vector.memset(self.onesb[:], 1.0), writes=[self.b_const])
        t.op("act", lambda: nc.scalar.activation(out=self.cact[:], in_=self.pv(PV_C, NCH), func=AF.Silu),
             reads=[self.b_pvec], writes=[self.b_cact])
        for j in range(2):
            t.op("dve", lambda j=j: nc.vector.tensor_scalar(out=self.negsink[:, j, :], in0=self.pv(PV_ATT + j * PV_ASZ, NHEADS),
                                                            scalar1=-1.0, scalar2=None, op0=ALU.mult),
                 reads=[self.b_pvec], writes=[self.b_mod])

    def phase_mod(self):
        t = self.t
        nc = self.nc
        for l in range(self.depth):
            ps, b_ps = self.ps_next()
            for og in range(24):
                src = self.w_ada[l][:, og * 512:(og + 1) * 512].rearrange("(k p) n -> p k n", p=128)
                wb, b_wb = self.wnext([(0, NCH, 512, src)])
                w3 = wb[:, 0:NCH * 512].rearrange("p (k n) -> p k n", k=NCH)
                for oc in range(4):
                    col = og * 4 + oc
                    for k in range(NCH):
                        first = (og == 0 and oc == 0 and k == 0)
                        t.op("pe", lambda oc=oc, k=k, col=col: nc.tensor.matmul(
                            ps[:, col:col + 1], lhsT=w3[:, k, oc * 128:(oc + 1) * 128], rhs=self.cact[:, k:k + 1],
                            start=(k == 0), stop=(k == NCH - 1)),
                            reads=[b_wb, self.b_cact], writes=([b_ps] if first else []),
                            signal=(k == NCH - 1 and oc == 3))
                self.wrelease()
            if not t.dry:
                b_ps.setw(("pe", t.cnt["pe"]))
            bada = self.pv(PV_L + l * PV_LSZ, 96)
            ng = self.pv(PV_L + l * PV_LSZ + 96, 64).rearrange("p (a b) -> p a b", a=4)
            md = self.mod[:, l, :]
            t.op("dve", lambda: nc.vector.tensor_tensor(out=md, in0=ps[:, 0:96], in1=bada, op=ALU.add),
                 reads=[b_ps, self.b_pvec], writes=[self.b_mod])
            for (dst, gi, mi, plus1) in ((0, 0, 1, True), (1, 1, 2, False), (2, 2, 4, True), (3, 3, 5, False)):
                if plus1:
                    t.op("dve", lambda dst=dst, gi=gi, mi=mi: nc.vector.scalar_tensor_tensor(
                        out=self.prm[:, l, dst, :], in0=md[:, mi * 16:(mi + 1) * 16], scalar=1.0, in1=ng[:, gi, :],
                        op0=ALU.add, op1=ALU.mult), reads=[self.b_mod, self.b_pvec], writes=[self.b_mod])
                else:
                    t.op("dve", lambda dst=dst, gi=gi, mi=mi: nc.vector.tensor_tensor(
                        out=self.prm[:, l, dst, :], in0=md[:, mi * 16:(mi + 1) * 16], in1=ng[:, gi, :], op=ALU.mult),
                        reads=[self.b_mod, self.b_pvec], writes=[self.b_mod])

    def phase_t0(self):
        t = self.t
        nc = self.nc
        self.arena_reset()
        xin = [self.cf32([D]) for _ in range(2)]
        b_xin = [self.newbuf("xin") for _ in range(2)]
        xo = [self.cf32([NCH, 512]) for _ in range(2)]
        b_xo = [self.newbuf("xo") for _ in range(2)]
        nblk_tot = NBLK
        for b in range(nblk_tot):
            s = b % 2
            g4 = (b // 4) % 2
            t.dma("sp", [(xin[s], self.x_in[b * 128:(b + 1) * 128, :])], reads=[self.b_in], writes=[b_xin[s]], sembuf=b_xin[s])
            for cg in range(4):
                ps, b_ps = self.ps_next()
                for j in range(4):
                    c = cg * 4 + j
                    t.op("pe", lambda c=c, j=j, ps=ps: nc.tensor.transpose(ps[:, j * 128:(j + 1) * 128], xin[s][:, c * 128:(c + 1) * 128],
                                                                            self.identf[:]),
                         reads=[b_xin[s], self.b_const], writes=([b_ps] if j == 0 else []), signal=(j == 3))
                if not t.dry:
                    b_ps.setw(("pe", t.cnt["pe"]))
                dst = xo[g4][:, cg * 4:(cg + 1) * 4, (b % 4) * 128:(b % 4) * 128 + 128]
                src = ps[:, :].rearrange("p (a b) -> p a b", a=4)
                if cg % 2 == 0:
                    t.op("dve", lambda dst=dst, src=src: nc.vector.tensor_copy(out=dst, in_=src), reads=[b_ps], writes=[b_xo[g4]])
                else:
                    t.op("act", lambda dst=dst, src=src: nc.scalar.copy(out=dst, in_=src), reads=[b_ps], writes=[b_xo[g4]])
            if b % 4 == 3:
                tok0 = (b - 3) * 128
                self.store_x(xo[g4], b_xo[g4], tok0, 512)
        t.barrier()

    def phase_c1(self, l, j):
        t = self.t
        nc = self.nc
        r = self.depth - 1 - l
        b0, b1 = BASEB - r - 1, NBLK - BASEB + r + 1
        self.arena_reset()
        nmax = SUBB * 128
        x = self.cf32([NCH, nmax])
        b_x = self.newbuf("x")
        h = [self.cbf([NCH, nmax]) for _ in range(NSUB)]
        b_h = [self.newbuf("h") for _ in range(NSUB)]
        u = [self.cf32([NCH, nmax]) for _ in range(NSUB)]
        b_u = [self.newbuf("u") for _ in range(NSUB)]
        tm = [self.carve(nmax * 4, F32, [nmax]) for _ in range(NSUB)]
        b_tm = [self.newbuf("tm") for _ in range(NSUB)]
        sig = self.cf32([3, nmax])
        b_sig = [self.newbuf("sig") for _ in range(3)]
        ut = self.cf32([3, nmax])
        b_ut = [self.newbuf("ut") for _ in range(3)]
        scr = self.norm_scratch(nmax)
        a_m = self.prm[:, l, 0, :]
        sh_m = self.mod[:, l, 0:16]
        w1 = self.w_pw1[j]
        si = 0
        for st in supertiles(b0, b1):
            for s, (tok0, n) in enumerate(st):
                self.load_x(x, b_x, tok0, n)
                t.dma("sp", [(tm[s][:, 0:n], self.tokmask_in[:, tok0:tok0 + n])], reads=[self.b_in], writes=[b_tm[s]], sembuf=b_tm[s])
                self.prenorm(x, b_x, n, a_m, sh_m, h[s], b_h[s], scr)
            for mg in range(8):
                srca = w1[:, mg * 256:(mg + 1) * 256].rearrange("(k p) n -> p k n", p=128)
                srcg = w1[:, D + mg * 256:D + (mg + 1) * 256].rearrange("(k p) n -> p k n", p=128)
                wb, b_wb = self.wnext([(0, NCH, 256, srca), (NCH * 256, NCH, 256, srcg)])
                wa = wb[:, 0:NCH * 256].rearrange("p (k n) -> p k n", k=NCH)
                wg = wb[:, NCH * 256:2 * NCH * 256].rearrange("p (k n) -> p k n", k=NCH)
                for mm in range(2):
                    m = mg * 2 + mm
                    for s, (tok0, n) in enumerate(st):
                        psa, b_psa = self.ps_next()
                        psg, b_psg = self.ps_next()
                        for (ps, b_ps, w3) in ((psg, b_psg, wg), (psa, b_psa, wa)):
                            for k in range(NCH):
                                t.op("pe", lambda ps=ps, w3=w3, k=k, s=s, n=n, mm=mm: nc.tensor.matmul(
                                    ps[:, 0:n], lhsT=w3[:, k, mm * 128:(mm + 1) * 128], rhs=h[s][:, k, 0:n],
                                    start=(k == 0), stop=(k == NCH - 1)),
                                    reads=[b_wb, b_h[s]], writes=([b_ps] if k == 0 else []), signal=(k == NCH - 1))
                            if not t.dry:
                                b_ps.setw(("pe", t.cnt["pe"]))
                        q = si % 3
                        si += 1
                        t.op("act", lambda q=q, n=n, psg=psg: nc.scalar.activation(out=sig[:, q, 0:n], in_=psg[:, 0:n], func=AF.Sigmoid),
                             reads=[b_psg], writes=[b_sig[q]])
                        t.op("dve", lambda q=q, n=n, psa=psa: nc.vector.tensor_tensor(out=ut[:, q, 0:n], in0=psa[:, 0:n], in1=sig[:, q, 0:n],
                                                                                      op=ALU.mult),
                             reads=[b_psa, b_sig[q]], writes=[b_ut[q]])
                        t.op("pool", lambda q=q, n=n, s=s, m=m: nc.gpsimd.tensor_tensor(out=u[s][:, m, 0:n], in0=ut[:, q, 0:n], in1=tm[s][:, 0:n],
                                                                                        op=ALU.mult),
                             reads=[b_ut[q], b_tm[s]], writes=[b_u[s]])
                self.wrelease()
            for s, (tok0, n) in enumerate(st):
                t.dma("sp", [(self.us[:, :, tok0:tok0 + n].rearrange("c p t -> p c t"), u[s][:, :, 0:n])],
                      reads=[b_u[s]], writes=[self.b_us], sembuf=b_u[s])
        t.barrier()

    def mix_residual(self, l, o, b_o, n, x, b_x, scr, tok0):
        t = self.t
        nc = self.nc
        ps, b_ps = self.sumsq(o, b_o, n, scr["sq"], scr["b_sq"])
        self.rstd_from_ps(ps, b_ps, n, scr["rstd"], scr["b_rstd"], scr["tmp"], scr["b_tmp"])
        gm = self.prm[:, l, 1, :]
        rstd = scr["rstd"]
        for k in range(NCH):
            t.op("dve", lambda k=k: nc.vector.scalar_tensor_tensor(out=o[:, k, 0:n], in0=o[:, k, 0:n], scalar=gm[:, k:k + 1],
                                                                   in1=rstd[:, 0:n], op0=ALU.mult, op1=ALU.mult),
                 reads=[b_o, scr["b_rstd"], self.b_mod], writes=[b_o])
            t.op("pool", lambda k=k: nc.gpsimd.tensor_tensor(out=x[:, k, 0:n], in0=x[:, k, 0:n], in1=o[:, k, 0:n], op=ALU.add),
                 reads=[b_o, b_x], writes=[b_x])
        self.store_x(x, b_x, tok0, n)

    def phase_c2(self, l, j):
        t = self.t
        nc = self.nc
        r = self.depth - 1 - l
        b0, b1 = BASEB - r, NBLK - BASEB + r
        self.arena_reset()
        nmax = SUBB * 128
        HW = CW // 2
        uh = self.cf32([NCH, nmax + 2 * HW])
        b_uh = self.newbuf("uh")
        v = self.cf32([NCH, nmax])
        b_v = [self.newbuf("v") for _ in range(NCH)]
        z = self.cbf([NCH, nmax])
        b_z = self.newbuf("z")
        x = self.cf32([NCH, nmax])
        b_x = self.newbuf("x")
        vb = self.cbf([4, nmax])
        b_vb = [self.newbuf("vb") for _ in range(4)]
        mean = self.carve(nmax * 4, F32, [nmax])
        b_mean = self.newbuf("mean")
        msq = self.carve(nmax * 4, F32, [nmax])
        b_msq = self.newbuf("msq")
        scr = self.norm_scratch(nmax)
        pc = PV_CONV + j * PV_CSZ
        wdw = self.pv(pc, 496).rearrange("p (k c) -> p k c", k=CW)
        bdw = self.pv(pc + 496, 16)
        lng = self.pv(pc + 512, 16)
        lnb = self.pv(pc + 528, 16)
        w2 = self.w_pw2[j]
        o = v
        for (tok0, n) in subtiles(b0, b1):
            t.dma("sp", [(uh[:, :, 0:n + 2 * HW], self.us[:, :, tok0 - HW:tok0 + n + HW].rearrange("c p t -> p c t"))],
                  reads=[self.b_us], writes=[b_uh], sembuf=b_uh)
            self.load_x(x, b_x, tok0, n)
            for k in range(CW):
                for c in range(NCH):
                    if k == 0:
                        t.op("dve", lambda c=c: nc.vector.tensor_scalar(out=v[:, c, 0:n], in0=uh[:, c, 0:n], scalar1=wdw[:, 0, c:c + 1],
                                                                        scalar2=bdw[:, c:c + 1], op0=ALU.mult, op1=ALU.add),
                             reads=[b_uh, self.b_pvec], writes=[b_v[c]])
                    else:
                        t.op("dve", lambda c=c, k=k: nc.vector.scalar_tensor_tensor(out=v[:, c, 0:n], in0=uh[:, c, k:k + n],
                                                                                    scalar=wdw[:, k, c:c + 1], in1=v[:, c, 0:n],
                                                                                    op0=ALU.mult, op1=ALU.add),
                             reads=[b_uh, b_v[c]], writes=[b_v[c]])
            psm, b_psm = self.ps_next()
            pss, b_pss = self.ps_next()
            for c in range(NCH):
                q = c % 4
                t.op("pool", lambda c=c, q=q: nc.gpsimd.tensor_copy(out=vb[:, q, 0:n], in_=v[:, c, 0:n]), reads=[b_v[c]], writes=[b_vb[q]])
                t.op("pe", lambda c=c, q=q: nc.tensor.matmul(psm[:, 0:n], lhsT=self.onesb[:], rhs=vb[:, q, 0:n], start=(c == 0), stop=(c == NCH - 1)),
                     reads=[b_vb[q], self.b_const], writes=([b_psm] if c == 0 else []), signal=True)
                q2 = c % 4
                t.op("act", lambda c=c, q2=q2: nc.scalar.activation(out=scr["sq"][:, q2, 0:n], in_=v[:, c, 0:n], func=AF.Square),
                     reads=[b_v[c]], writes=[scr["b_sq"][q2]])
                t.op("pe", lambda c=c, q2=q2: nc.tensor.matmul(pss[:, 0:n], lhsT=self.onesb[:], rhs=scr["sq"][:, q2, 0:n], start=(c == 0),
                                                             stop=(c == NCH - 1)),
                     reads=[scr["b_sq"][q2], self.b_const], writes=([b_pss] if c == 0 else []), signal=True)
            if not t.dry:
                b_psm.setw(("pe", t.cnt["pe"]))
                b_pss.setw(("pe", t.cnt["pe"]))
            t.op("dve", lambda: nc.vector.tensor_scalar(out=mean[:, 0:n], in0=psm[:, 0:n], scalar1=1.0 / D, scalar2=None, op0=ALU.mult),
                 reads=[b_psm], writes=[b_mean])
            t.op("dve", lambda: nc.vector.tensor_tensor(out=msq[:, 0:n], in0=mean[:, 0:n], in1=mean[:, 0:n], op=ALU.mult),
                 reads=[b_mean], writes=[b_msq])
            tmp = scr["tmp"]
            t.op("dve", lambda: nc.vector.scalar_tensor_tensor(out=tmp[:, 0:n], in0=pss[:, 0:n], scalar=1.0 / D, in1=msq[:, 0:n],
                                                               op0=ALU.mult, op1=ALU.subtract),
                 reads=[b_pss, b_msq], writes=[scr["b_tmp"]])
            t.op("dve", lambda: nc.vector.tensor_scalar(out=tmp[:, 0:n], in0=tmp[:, 0:n], scalar1=EPS, scalar2=None, op0=ALU.add),
                 reads=[scr["b_tmp"]], writes=[scr["b_tmp"]])
            t.op("dve", lambda: nc.vector.reciprocal(out=tmp[:, 0:n], in_=tmp[:, 0:n]), reads=[scr["b_tmp"]], writes=[scr["b_tmp"]])
            rstd = scr["rstd"]
            t.op("act", lambda: nc.scalar.activation(out=rstd[:, 0:n], in_=tmp[:, 0:n], func=AF.Sqrt), reads=[scr["b_tmp"]],
                 writes=[scr["b_rstd"]])
            for c in range(NCH):
                t.op("dve", lambda c=c: nc.vector.tensor_tensor(out=v[:, c, 0:n], in0=v[:, c, 0:n], in1=mean[:, 0:n], op=ALU.subtract),
                     reads=[b_v[c], b_mean], writes=[b_v[c]])
                t.op("pool", lambda c=c: nc.gpsimd.tensor_tensor(out=v[:, c, 0:n], in0=v[:, c, 0:n], in1=rstd[:, 0:n], op=ALU.mult),
                     reads=[b_v[c], scr["b_rstd"]], writes=[b_v[c]])
                t.op("act", lambda c=c: nc.scalar.activation(out=z[:, c, 0:n], in_=v[:, c, 0:n], func=AF.Silu, bias=lnb[:, c:c + 1],
                                                             scale=lng[:, c:c + 1]),
                     reads=[b_v[c], self.b_pvec], writes=[b_z])
            for mg in range(4):
                src = w2[:, mg * 512:(mg + 1) * 512].rearrange("(k p) n -> p k n", p=128)
                wb, b_wb = self.wnext([(0, NCH, 512, src)])
                w3 = wb[:, 0:NCH * 512].rearrange("p (k n) -> p k n", k=NCH)
                for mm in range(4):
                    m = mg * 4 + mm
                    ps, b_ps = self.ps_next()
                    for k in range(NCH):
                        t.op("pe", lambda ps=ps, k=k, mm=mm: nc.tensor.matmul(ps[:, 0:n], lhsT=w3[:, k, mm * 128:(mm + 1) * 128], rhs=z[:, k, 0:n],
                                                                              start=(k == 0), stop=(k == NCH - 1)),
                             reads=[b_wb, b_z], writes=([b_ps] if k == 0 else []), signal=(k == NCH - 1))
                    if not t.dry:
                        b_ps.setw(("pe", t.cnt["pe"]))
                    if m % 2 == 0:
                        t.op("dve", lambda ps=ps, m=m: nc.vector.tensor_copy(out=o[:, m, 0:n], in_=ps[:, 0:n]), reads=[b_ps], writes=[b_v[m]])
                    else:
                        t.op("act", lambda ps=ps, m=m: nc.scalar.copy(out=o[:, m, 0:n], in_=ps[:, 0:n]), reads=[b_ps], writes=[b_v[m]])
                self.wrelease()
            b_o = self.newbuf("o")
            if not t.dry:
                b_o.multi = True
                for c in range(NCH):
                    for ev in b_v[c].wevents():
                        b_o.setw(ev)
            self.mix_residual_multi(l, o, b_o, b_v, n, x, b_x, scr, tok0)
        t.barrier()

    def mix_residual_multi(self, l, o, b_o, b_oc, n, x, b_x, scr, tok0):
        t = self.t
        nc = self.nc
        ps, b_ps = self.ps_next()
        sq, b_sq = scr["sq"], scr["b_sq"]
        for k in range(NCH):
            jq = k % 4
            t.op("act", lambda k=k, jq=jq: nc.scalar.activation(out=sq[:, jq, 0:n], in_=o[:, k, 0:n], func=AF.Square),
                 reads=[b_oc[k]], writes=[b_sq[jq]])
            t.op("pe", lambda k=k, jq=jq: nc.tensor.matmul(ps[:, 0:n], lhsT=self.onesb[:], rhs=sq[:, jq, 0:n], start=(k == 0), stop=(k == NCH - 1)),
                 reads=[b_sq[jq], self.b_const], writes=([b_ps] if k == 0 else []), signal=True)
        if not t.dry:
            b_ps.setw(("pe", t.cnt["pe"]))
        self.rstd_from_ps(ps, b_ps, n, scr["rstd"], scr["b_rstd"], scr["tmp"], scr["b_tmp"])
        gm = self.prm[:, l, 1, :]
        rstd = scr["rstd"]
        for k in range(NCH):
            t.op("dve", lambda k=k: nc.vector.scalar_tensor_tensor(out=o[:, k, 0:n], in0=o[:, k, 0:n], scalar=gm[:, k:k + 1],
                                                                   in1=rstd[:, 0:n], op0=ALU.mult, op1=ALU.mult),
                 reads=[b_oc[k], scr["b_rstd"], self.b_mod], writes=[b_oc[k]])
            t.op("pool", lambda k=k: nc.gpsimd.tensor_tensor(out=x[:, k, 0:n], in0=x[:, k, 0:n], in1=o[:, k, 0:n], op=ALU.add),
                 reads=[b_oc[k], b_x], writes=[b_x])
        self.store_x(x, b_x, tok0, n)

    def phase_ffn(self, l, last):
        t = self.t
        nc = self.nc
        r = self.depth - 1 - l
        b0, b1 = BASEB - r, NBLK - BASEB + r
        self.arena_reset()
        nmax = SUBB * 128
        FH = FCH // NH
        h2 = [self.cbf([NCH, nmax]) for _ in range(NSUB)]
        b_h2 = [self.newbuf("h2") for _ in range(NSUB)]
        y = [self.cf32([NCH, nmax]) for _ in range(NSUB)]
        b_y = [[self.newbuf("y") for _ in range(NCH)] for _ in range(NSUB)]
        act = [self.cbf([FH, nmax]) for _ in range(NSUB)]
        b_act = [[self.newbuf("act") for _ in range(FH)] for _ in range(NSUB)]
        x = self.cf32([NCH, nmax])
        b_x = self.newbuf("x")
        sg = self.cf32([3, nmax])
        b_sg = [self.newbuf("sg") for _ in range(3)]
        scr = self.norm_scratch(nmax)
        if last:
            yof = act[0].rearrange("p a b -> p (a b)").bitcast(F32)
            yo = [yof[:, 0:D], yof[:, D:2 * D]]
            b_yo = [self.newbuf("yo") for _ in range(2)]
        a_f = self.prm[:, l, 2, :]
        sh_f = self.mod[:, l, 48:64]
        gf = self.prm[:, l, 3, :]
        wg_d, wu_d, wd_d = self.w_gate[l], self.w_up[l], self.w_down[l]
        si = 0
        yoi = 0
        for st in supertiles(b0, b1):
            for s, (tok0, n) in enumerate(st):
                self.load_x(x, b_x, tok0, n)
                self.prenorm(x, b_x, n, a_f, sh_f, h2[s], b_h2[s], scr)
            for hf in range(NH):
                mlist = list(range(hf * FH, (hf + 1) * FH))
                for g0 in range(0, FH, 2):
                    ms = mlist[g0:g0 + 2]
                    nm = len(ms)
                    c0 = ms[0] * 128
                    srcg = wg_d[:, c0:c0 + nm * 128].rearrange("(k p) n -> p k n", p=128)
                    srcu = wu_d[:, c0:c0 + nm * 128].rearrange("(k p) n -> p k n", p=128)
                    wb, b_wb = self.wnext([(0, NCH, nm * 128, srcg), (NCH * 256, NCH, nm * 128, srcu)])
                    wg3 = wb[:, 0:NCH * nm * 128].rearrange("p (k n) -> p k n", k=NCH)
                    wu3 = wb[:, NCH * 256:NCH * 256 + NCH * nm * 128].rearrange("p (k n) -> p k n", k=NCH)
                    for mi, m in enumerate(ms):
                        ml = m - hf * FH
                        for s, (tok0, n) in enumerate(st):
                            psg, b_psg = self.ps_next()
                            psu, b_psu = self.ps_next()
                            for (ps, b_ps, w3) in ((psg, b_psg, wg3), (psu, b_psu, wu3)):
                                for k in range(NCH):
                                    t.op("pe", lambda ps=ps, w3=w3, k=k, s=s, n=n, mi=mi: nc.tensor.matmul(
                                        ps[:, 0:n], lhsT=w3[:, k, mi * 128:(mi + 1) * 128], rhs=h2[s][:, k, 0:n],
                                        start=(k == 0), stop=(k == NCH - 1)),
                                        reads=[b_wb, b_h2[s]], writes=([b_ps] if k == 0 else []), signal=(k == NCH - 1))
                                if not t.dry:
                                    b_ps.setw(("pe", t.cnt["pe"]))
                            q = si % 3
                            si += 1
                            t.op("act", lambda q=q, n=n, psg=psg: nc.scalar.activation(out=sg[:, q, 0:n], in_=psg[:, 0:n], func=AF.Silu),
                                 reads=[b_psg], writes=[b_sg[q]])
                            t.op("dve", lambda q=q, n=n, psu=psu, s=s, ml=ml: nc.vector.tensor_tensor(
                                out=act[s][:, ml, 0:n], in0=psu[:, 0:n], in1=sg[:, q, 0:n], op=ALU.mult),
                                reads=[b_psu, b_sg[q]], writes=[b_act[s][ml]])
                    self.wrelease()
                for og in range(NCH // 2):
                    src = wd_d[hf * FH * 128:(hf + 1) * FH * 128, og * 256:(og + 1) * 256].rearrange("(k p) n -> p k n", p=128)
                    wb, b_wb = self.wnext([(0, FH, 256, src)])
                    wd3 = wb[:, 0:FH * 256].rearrange("p (k n) -> p k n", k=FH)
                    for oc in range(2):
                        m = og * 2 + oc
                        for s, (tok0, n) in enumerate(st):
                            ps, b_ps = self.ps_next()
                            for k in range(FH):
                                t.op("pe", lambda ps=ps, k=k, s=s, n=n, oc=oc: nc.tensor.matmul(
                                    ps[:, 0:n], lhsT=wd3[:, k, oc * 128:(oc + 1) * 128], rhs=act[s][:, k, 0:n],
                                    start=(k == 0), stop=(k == FH - 1)),
                                    reads=[b_wb, b_act[s][k]], writes=([b_ps] if k == 0 else []), signal=(k == FH - 1))
                            if not t.dry:
                                b_ps.setw(("pe", t.cnt["pe"]))
                            if hf == 0:
                                t.op("act", lambda ps=ps, s=s, m=m, n=n: nc.scalar.copy(out=y[s][:, m, 0:n], in_=ps[:, 0:n]),
                                     reads=[b_ps], writes=[b_y[s][m]])
                            else:
                                t.op("dve", lambda ps=ps, s=s, m=m, n=n: nc.vector.tensor_tensor(out=y[s][:, m, 0:n], in0=ps[:, 0:n],
                                                                                               in1=y[s][:, m, 0:n], op=ALU.add),
                                     reads=[b_ps, b_y[s][m]], writes=[b_y[s][m]])
                    self.wrelease()
            for s, (tok0, n) in enumerate(st):
                ps, b_ps = self.ps_next()
                sq, b_sq = scr["sq"], scr["b_sq"]
                for k in range(NCH):
                    jq = k % 4
                    t.op("act", lambda k=k, jq=jq, s=s, n=n: nc.scalar.activation(out=sq[:, jq, 0:n], in_=y[s][:, k, 0:n], func=AF.Square),
                         reads=[b_y[s][k]], writes=[b_sq[jq]])
                    t.op("pe", lambda k=k, jq=jq, n=n, ps=ps: nc.tensor.matmul(ps[:, 0:n], lhsT=self.onesb[:], rhs=sq[:, jq, 0:n], start=(k == 0),
                                                                             stop=(k == NCH - 1)),
                         reads=[b_sq[jq], self.b_const], writes=([b_ps] if k == 0 else []), signal=True)
                if not t.dry:
                    b_ps.setw(("pe", t.cnt["pe"]))
                self.rstd_from_ps(ps, b_ps, n, scr["rstd"], scr["b_rstd"], scr["tmp"], scr["b_tmp"])
                rstd = scr["rstd"]
                self.load_x(x, b_x, tok0, n)
                for k in range(NCH):
                    t.op("dve", lambda k=k, s=s, n=n: nc.vector.scalar_tensor_tensor(out=y[s][:, k, 0:n], in0=y[s][:, k, 0:n], scalar=gf[:, k:k + 1],
                                                                                   in1=rstd[:, 0:n], op0=ALU.mult, op1=ALU.mult),
                         reads=[b_y[s][k], scr["b_rstd"], self.b_mod], writes=[b_y[s][k]])
                    t.op("pool", lambda k=k, s=s, n=n: nc.gpsimd.tensor_tensor(out=x[:, k, 0:n], in0=x[:, k, 0:n], in1=y[s][:, k, 0:n], op=ALU.add),
                         reads=[b_y[s][k], b_x], writes=[b_x])
                if not last:
                    self.store_x(x, b_x, tok0, n)
                else:
                    for bb in range(n // 128):
                        tokb = tok0 + bb * 128 - BASEB * 128
                        qy = yoi % 2
                        yoi += 1
                        for cg in range(4):
                            ps, b_ps = self.ps_next()
                            for jj in range(4):
                                c = cg * 4 + jj
                                t.op("pe", lambda c=c, jj=jj, ps=ps, bb=bb: nc.tensor.transpose(
                                    ps[:, jj * 128:(jj + 1) * 128], x[:, c, bb * 128:(bb + 1) * 128], self.identf[:]),
                                    reads=[b_x, self.b_const], writes=([b_ps] if jj == 0 else []), signal=(jj == 3))
                            if not t.dry:
                                b_ps.setw(("pe", t.cnt["pe"]))
                            if cg % 2 == 0:
                                t.op("dve", lambda ps=ps, cg=cg, qy=qy: nc.vector.tensor_copy(out=yo[qy][:, cg * 512:(cg + 1) * 512], in_=ps[:, :]),
                                     reads=[b_ps], writes=[b_yo[qy]])
                            else:
                                t.op("act", lambda ps=ps, cg=cg, qy=qy: nc.scalar.copy(out=yo[qy][:, cg * 512:(cg + 1) * 512], in_=ps[:, :]),
                                     reads=[b_ps], writes=[b_yo[qy]])
                        t.dma("sp", [(self.y_out[tokb:tokb + 128, :], yo[qy])], reads=[b_yo[qy]] + b_act[0], writes=[], sembuf=b_yo[qy])
        t.barrier()

    def phase_a1(self, l, j, rng=None):
        t = self.t
        nc = self.nc
        r = self.depth - 1 - l
        b0, b1 = BASEB - r - 1, NBLK - BASEB + r + 1
        if rng:
            b0, b1 = rng
        self.arena_reset()
        nmax = SUBB * 128
        x = self.cf32([NCH, nmax])
        b_x = self.newbuf("x")
        h = [self.cbf([NCH, nmax]) for _ in range(NSUB)]
        b_h = [self.newbuf("h") for _ in range(NSUB)]
        qT = [self.cbf([NCH, nmax]) for _ in range(NSUB)]
        b_qT = [self.newbuf("qT") for _ in range(NSUB)]
        kT = [self.cbf([NKV, nmax]) for _ in range(NSUB)]
        b_kT = [self.newbuf("kT") for _ in range(NSUB)]
        vt = [self.cbf([SUBB, 512]) for _ in range(NSUB)]
        b_vt = [self.newbuf("vt") for _ in range(NSUB)]
        rc = [self.carve(nmax * 4, F32, [nmax]) for _ in range(NSUB)]
        rs = [self.carve(nmax * 4, F32, [nmax]) for _ in range(NSUB)]
        b_rcs = [self.newbuf("rcs") for _ in range(NSUB)]
        qb = self.cbf([2, nmax])
        b_qb = [self.newbuf("qb") for _ in range(2)]
        t1 = self.cf32([2, nmax])
        b_t1 = [self.newbuf("t1") for _ in range(2)]
        t2 = self.cf32([2, nmax])
        b_t2 = [self.newbuf("t2") for _ in range(2)]
        scr = self.norm_scratch(nmax)
        a_m = self.prm[:, l, 0, :]
        sh_m = self.mod[:, l, 0:16]
        ri = 0
        for st in supertiles(b0, b1):
            for s, (tok0, n) in enumerate(st):
                self.load_x(x, b_x, tok0, n)
                t.dma("sp", [(rc[s][:, 0:n], self.ropec_in[:, tok0:tok0 + n]), (rs[s][:, 0:n], self.ropes_in[:, tok0:tok0 + n])],
                      reads=[self.b_in], writes=[b_rcs[s]], sembuf=b_rcs[s])
                self.prenorm(x, b_x, n, a_m, sh_m, h[s], b_h[s], scr)
            for fi in range(5):
                if fi < 4:
                    src = self.w_q[j][:, fi * 512:(fi + 1) * 512].rearrange("(k p) n -> p k n", p=128)
                else:
                    src = self.w_k[j].rearrange("(k p) n -> p k n", p=128)
                wb, b_wb = self.wnext([(0, NCH, 512, src)])
                w3 = wb[:, 0:NCH * 512].rearrange("p (k n) -> p k n", k=NCH)
                for mm in range(4):
                    for s, (tok0, n) in enumerate(st):
                        if fi < 4:
                            dst, b_dst = qT[s][:, fi * 4 + mm, 0:n], b_qT[s]
                        else:
                            dst, b_dst = kT[s][:, mm, 0:n], b_kT[s]
                        ps, b_ps = self.ps_next()
                        for k in range(NCH):
                            t.op("pe", lambda ps=ps, k=k, s=s, n=n, mm=mm: nc.tensor.matmul(
                                ps[:, 0:n], lhsT=w3[:, k, mm * 128:(mm + 1) * 128], rhs=h[s][:, k, 0:n],
                                start=(k == 0), stop=(k == NCH - 1)),
                                reads=[b_wb, b_h[s]], writes=([b_ps] if k == 0 else []), signal=(k == NCH - 1))
                        if not t.dry:
                            b_ps.setw(("pe", t.cnt["pe"]))
                        q = ri % 2
                        ri += 1
                        if "rope" in DBG_SKIP:
                            t.op("act", lambda ps=ps, dst=dst: nc.scalar.copy(out=dst, in_=ps[:, 0:n]), reads=[b_ps], writes=[b_dst])
                            continue
                        t.op("act", lambda ps=ps, q=q, n=n: nc.scalar.copy(out=qb[:, q, 0:n], in_=ps[:, 0:n]), reads=[b_ps], writes=[b_qb[q]])
                        if "rotmm" in DBG_SKIP:
                            ps2, b_ps2 = ps, b_ps
                        else:
                            ps2, b_ps2 = self.ps_next()
                            t.op("pe", lambda ps2=ps2, q=q, n=n: nc.tensor.matmul(ps2[:, 0:n], lhsT=self.rotb[:], rhs=qb[:, q, 0:n], start=True, stop=True),
                                 reads=[b_qb[q], self.b_const], writes=[b_ps2])
                        t.op("dve", lambda ps=ps, q=q, n=n, s=s: nc.vector.tensor_tensor(out=t1[:, q, 0:n], in0=ps[:, 0:n], in1=rc[s][:, 0:n], op=ALU.mult),
                             reads=[b_ps, b_rcs[s]], writes=[b_t1[q]])
                        t.op("dve", lambda ps2=ps2, q=q, n=n, s=s: nc.vector.tensor_tensor(out=t2[:, q, 0:n], in0=ps2[:, 0:n], in1=rs[s][:, 0:n], op=ALU.mult),
                             reads=[b_ps2, b_rcs[s]], writes=[b_t2[q]])
                        t.op("dve", lambda dst=dst, q=q, n=n: nc.vector.tensor_tensor(out=dst, in0=t1[:, q, 0:n], in1=t2[:, q, 0:n], op=ALU.add),
                             reads=[b_t1[q], b_t2[q]], writes=[b_dst])
                self.wrelease()
            src = self.w_v[j].rearrange("(k p) n -> p k n", p=128)
            wb, b_wb = self.wnext([(0, NCH, 512, src)])
            w3 = wb[:, 0:NCH * 512].rearrange("p (k n) -> p k n", k=NCH)
            for s, (tok0, n) in enumerate(st):
                for bb in range(n // 128 if "v" not in DBG_SKIP.split(",") else 0):
                    ps, b_ps = self.ps_next()
                    for k in range(NCH):
                        t.op("pe", lambda ps=ps, k=k, s=s, bb=bb: nc.tensor.matmul(
                            ps[:, :], lhsT=h[s][:, k, bb * 128:(bb + 1) * 128], rhs=w3[:, k, :], start=(k == 0), stop=(k == NCH - 1)),
                            reads=[b_wb, b_h[s]], writes=([b_ps] if k == 0 else []), signal=(k == NCH - 1))
                    if not t.dry:
                        b_ps.setw(("pe", t.cnt["pe"]))
                    if bb % 2 == 0:
                        t.op("dve", lambda ps=ps, s=s, bb=bb: nc.vector.tensor_copy(out=vt[s][:, bb, :], in_=ps[:, :]), reads=[b_ps], writes=[b_vt[s]])
                    else:
                        t.op("act", lambda ps=ps, s=s, bb=bb: nc.scalar.copy(out=vt[s][:, bb, :], in_=ps[:, :]), reads=[b_ps], writes=[b_vt[s]])
            self.wrelease()
            for s, (tok0, n) in enumerate(st if "st" not in DBG_SKIP.split(",") else []):
                nb = n // 128
                t.dma("sp", [(self.qs[:, :, tok0:tok0 + n].rearrange("c p t -> p c t"), qT[s][:, :, 0:n])],
                      reads=[b_qT[s]], writes=[self.b_qs], sembuf=b_qT[s])
                t.dma("sp", [(self.ks[:, :, tok0:tok0 + n].rearrange("c p t -> p c t"), kT[s][:, :, 0:n])],
                      reads=[b_kT[s]], writes=[self.b_ks], sembuf=b_kT[s])
                t.dma("sp", [(self.vs[tok0:tok0 + n, :].rearrange("(b p) f -> p b f", p=128), vt[s][:, 0:nb, :])],
                      reads=[b_vt[s]], writes=[self.b_vs], sembuf=b_vt[s])
        t.barrier()

    def phase_a2(self, l, j, rng=None):
        t = self.t
        nc = self.nc
        r = self.depth - 1 - l
        b0, b1 = BASEB - r, NBLK - BASEB + r
        if rng:
            b0, b1 = rng
        self.arena_reset()
        nmax = SUBB * 128
        qT = self.cbf([NCH, nmax])
        b_qT = self.newbuf("qT")
        kT = self.cbf([NKV, nmax + 256])
        b_kT = self.newbuf("kT")
        vt = self.cbf([SUBB + 2, 512])
        b_vt = self.newbuf("vt")
        mk = self.cbf([SUBB, 384])
        b_mk = self.newbuf("mk")
        oT = self.cbf([NCH, nmax])
        b_oT = self.newbuf("oT")
        ao = self.cf32([NCH, nmax])
        b_ao = [self.newbuf("ao") for _ in range(NCH)]
        x = self.cf32([NCH, nmax])
        b_x = self.newbuf("x")
        p = self.cbf([4, 384])
        b_p = [self.newbuf("p") for _ in range(4)]
        pT = self.cbf([4, 384])
        b_pT = [self.newbuf("pT") for _ in range(4)]
        dg = self.cbf([4, 128])
        b_dg = [self.newbuf("dg") for _ in range(4)]
        sm = self.cf32([2, 6, 4])
        b_sm = [self.newbuf("sm") for _ in range(2)]
        scr = self.norm_scratch(nmax)
        wo = self.w_o[j]
        negsink = self.negsink[:, j, :]
        sink = self.pv(PV_ATT + j * PV_ASZ, NHEADS)
        gi = 0
        pi = 0
        for (tok0, n) in subtiles(b0, b1):
            nb = n // 128
            blk0 = tok0 // 128
            t.dma("sp", [(qT[:, :, 0:n], self.qs[:, :, tok0:tok0 + n].rearrange("c p t -> p c t"))],
                  reads=[self.b_qs], writes=[b_qT], sembuf=b_qT)
            t.dma("sp", [(kT[:, :, 0:n + 256], self.ks[:, :, tok0 - 128:tok0 + n + 128].rearrange("c p t -> p c t"))],
                  reads=[self.b_ks], writes=[b_kT], sembuf=b_kT)
            t.dma("sp", [(vt[:, 0:nb + 2, :], self.vs[tok0 - 128:tok0 + n + 128, :].rearrange("(b p) f -> p b f", p=128))],
                  reads=[self.b_vs], writes=[b_vt], sembuf=b_vt)
            t.dma("pool", [(mk[:, 0:nb, :], self.amask_in[blk0:blk0 + nb].rearrange("b p j -> p b j"))],
                  reads=[self.b_in], writes=[b_mk], sembuf=b_mk)
            self.load_x(x, b_x, tok0, n)
            for qbk in range(nb):
                for kvh in range(NKV):
                    g = gi % 2
                    gi += 1
                    smg = sm[:, g, :, :]
                    scs = []
                    for hh in range(4):
                        hd = kvh * 4 + hh
                        ps, b_ps = self.ps_next()
                        scs.append((ps, b_ps))
                        t.op("pe", lambda ps=ps, hd=hd, qbk=qbk, kvh=kvh: nc.tensor.matmul(
                            ps[:, 0:384], lhsT=qT[:, hd, qbk * 128:(qbk + 1) * 128], rhs=kT[:, kvh, qbk * 128:qbk * 128 + 384],
                            start=True, stop=False), reads=[b_qT, b_kT], writes=[b_ps], signal=False)
                        t.op("pe", lambda ps=ps, qbk=qbk: nc.tensor.matmul(ps[:, 0:384], lhsT=self.identb[:], rhs=mk[:, qbk, :], start=False, stop=True),
                             reads=[b_mk, self.b_const], writes=[], signal=True)
                        if not t.dry:
                            b_ps.setw(("pe", t.cnt["pe"]))
                        t.op("dve", lambda ps=ps, hh=hh, smg=smg: nc.vector.reduce_max(out=smg[:, 0, hh:hh + 1], in_=ps[:, 0:384], axis=AX.X),
                             reads=[b_ps], writes=[b_sm[g]])
                    t.op("dve", lambda smg=smg: nc.vector.tensor_scalar(out=smg[:, 1, :], in0=smg[:, 0, :], scalar1=-SCALE, scalar2=None, op0=ALU.mult),
                         reads=[b_sm[g]], writes=[b_sm[g]])
                    t.op("dve", lambda smg=smg, kvh=kvh: nc.vector.tensor_tensor(out=smg[:, 1, :], in0=smg[:, 1, :], in1=negsink[:, kvh * 4:kvh * 4 + 4],
                                                                                 op=ALU.min),
                         reads=[b_sm[g], self.b_mod], writes=[b_sm[g]])
                    pis = []
                    for hh in range(4):
                        hd = kvh * 4 + hh
                        ps, b_ps = scs[hh]
                        q = pi % 4
                        pi += 1
                        pis.append(q)
                        t.op("act", lambda ps=ps, q=q, hh=hh, smg=smg: nc.scalar.activation(
                            out=p[:, q, :], in_=ps[:, 0:384], func=AF.Exp, bias=smg[:, 1, hh:hh + 1], scale=SCALE,
                            accum_out=smg[:, 2, hh:hh + 1]), reads=[b_ps, b_sm[g]], writes=[b_p[q], b_sm[g]])
                        t.op("act", lambda hh=hh, hd=hd, smg=smg: nc.scalar.activation(
                            out=smg[:, 3, hh:hh + 1], in_=smg[:, 1, hh:hh + 1], func=AF.Exp, bias=sink[:, hd:hd + 1], scale=1.0),
                            reads=[b_sm[g], self.b_pvec], writes=[b_sm[g]])
                    t.op("dve", lambda smg=smg: nc.vector.tensor_tensor(out=smg[:, 4, :], in0=smg[:, 2, :], in1=smg[:, 3, :], op=ALU.add),
                         reads=[b_sm[g]], writes=[b_sm[g]])
                    t.op("dve", lambda smg=smg: nc.vector.reciprocal(out=smg[:, 5, :], in_=smg[:, 4, :]), reads=[b_sm[g]], writes=[b_sm[g]])
                    pso, b_pso = self.ps_next()
                    for hh in range(4):
                        hd = kvh * 4 + hh
                        q = pis[hh]
                        t.op("dve", lambda q=q, hh=hh, smg=smg: nc.vector.tensor_scalar(out=dg[:, q, :], in0=self.identb[:], scalar1=smg[:, 5, hh:hh + 1],
                                                                                      scalar2=None, op0=ALU.mult),
                             reads=[b_sm[g], self.b_const], writes=[b_dg[q]])
                        pst, b_pst = self.ps_next()
                        for kb in range(3):
                            t.op("pe", lambda pst=pst, q=q, kb=kb: nc.tensor.matmul(pst[:, kb * 128:(kb + 1) * 128], lhsT=p[:, q, kb * 128:(kb + 1) * 128],
                                                                                    rhs=dg[:, q, :], start=True, stop=True),
                                 reads=[b_p[q], b_dg[q]], writes=([b_pst] if kb == 0 else []), signal=(kb == 2))
                        if not t.dry:
                            b_pst.setw(("pe", t.cnt["pe"]))
                        if hh % 2 == 0:
                            t.op("dve", lambda pst=pst, q=q: nc.vector.tensor_copy(out=pT[:, q, :], in_=pst[:, 0:384]), reads=[b_pst], writes=[b_pT[q]])
                        else:
                            t.op("act", lambda pst=pst, q=q: nc.scalar.copy(out=pT[:, q, :], in_=pst[:, 0:384]), reads=[b_pst], writes=[b_pT[q]])
                        for kb in range(3):
                            first = (hh == 0 and kb == 0)
                            t.op("pe", lambda pso=pso, q=q, kb=kb, hh=hh, kvh=kvh, qbk=qbk: nc.tensor.matmul(
                                pso[:, hh * 128:(hh + 1) * 128], lhsT=vt[:, qbk + kb, kvh * 128:(kvh + 1) * 128], rhs=pT[:, q, kb * 128:(kb + 1) * 128],
                                start=(kb == 0), stop=(kb == 2)), reads=[b_vt, b_pT[q]], writes=([b_pso] if first else []),
                                signal=(kb == 2))
                    if not t.dry:
                        b_pso.setw(("pe", t.cnt["pe"]))
                    dsto = oT[:, kvh * 4:(kvh + 1) * 4, qbk * 128:(qbk + 1) * 128]
                    srco = pso[:, :].rearrange("p (a b) -> p a b", a=4)
                    if kvh % 2 == 0:
                        t.op("act", lambda dsto=dsto, srco=srco: nc.scalar.copy(out=dsto, in_=srco), reads=[b_pso], writes=[b_oT])
                    else:
                        t.op("dve", lambda dsto=dsto, srco=srco: nc.vector.tensor_copy(out=dsto, in_=srco), reads=[b_pso], writes=[b_oT])
            for mg in range(4):
                src = wo[:, mg * 512:(mg + 1) * 512].rearrange("(k p) n -> p k n", p=128)
                wb, b_wb = self.wnext([(0, NCH, 512, src)])
                w3 = wb[:, 0:NCH * 512].rearrange("p (k n) -> p k n", k=NCH)
                for mm in range(4):
                    m = mg * 4 + mm
                    ps, b_ps = self.ps_next()
                    for k in range(NCH):
                        t.op("pe", lambda ps=ps, k=k, mm=mm: nc.tensor.matmul(ps[:, 0:n], lhsT=w3[:, k, mm * 128:(mm + 1) * 128], rhs=oT[:, k, 0:n],
                                                                              start=(k == 0), stop=(k == NCH - 1)),
                             reads=[b_wb, b_oT], writes=([b_ps] if k == 0 else []), signal=(k == NCH - 1))
                    if not t.dry:
                        b_ps.setw(("pe", t.cnt["pe"]))
                    if m % 2 == 0:
                        t.op("dve", lambda ps=ps, m=m: nc.vector.tensor_copy(out=ao[:, m, 0:n], in_=ps[:, 0:n]), reads=[b_ps], writes=[b_ao[m]])
                    else:
                        t.op("act", lambda ps=ps, m=m: nc.scalar.copy(out=ao[:, m, 0:n], in_=ps[:, 0:n]), reads=[b_ps], writes=[b_ao[m]])
                self.wrelease()
            self.mix_residual_multi(l, ao, None, b_ao, n, x, b_x, scr, tok0)
        t.barrier()

    def emit_dbg(self):
        t = self.t
        nc = self.nc
        self.phase_init()
        t.op("dve", lambda: nc.vector.memset(self.mod[:], 0.0), writes=[self.b_mod])
        t.op("dve", lambda: nc.vector.memset(self.prm[:], 1.0), writes=[self.b_mod])
        self.phase_t0()
        if self.dbg != "t0":
            self.phase_a1(1, 0, rng=(3, 9))
        if self.dbg == "attn":
            self.phase_a2(1, 0, rng=(4, 7))

    def emit_all(self):
        if self.dbg:
            return self.emit_dbg()
        self.phase_init()
        self.phase_mod()
        self.phase_t0()
        for l in range(self.depth):
            j = l // 2
            if l % 2 == 0:
                self.phase_c1(l, j)
                self.phase_c2(l, j)
            else:
                self.phase_a1(l, j)
                self.phase_a2(l, j)
            self.phase_ffn(l, last=(l == self.depth - 1))

    def build(self):
        self.t.dry = True
        self.emit_all()
        self.t.dry = False
        self.ps_i = 0
        self.nbuf = 0
        self.emit_all()
        self.t.barrier()
        return self.nc


def _core_layout():
    lay = []
    for c in range(4):
        lay.append(("p", 0, c * TOK_CORE))
    for b in range(2):
        for c in range(2):
            lay.append(("s", b, c * TOK_CORE))
    return lay


def _host_inputs(inputs, depth):
    f32 = np.float32
    xp = np.asarray(inputs["x_prompt"], f32)
    xsm = np.asarray(inputs["x_sample"], f32)
    cp = np.asarray(inputs["c_prompt"], f32)
    cs = np.asarray(inputs["c_sample"], f32)
    b_ada = np.asarray(inputs["b_ada"], f32)
    norm_g = np.asarray(inputs["norm_g"], f32)
    wdw = np.asarray(inputs["conv_w_dw"], f32)
    bdw = np.asarray(inputs["conv_b_dw"], f32)
    lng = np.asarray(inputs["conv_ln_g"], f32)
    lnb = np.asarray(inputs["conv_ln_b"], f32)
    sink = np.asarray(inputs["attn_sink"], f32)

    def fm(vec):
        return np.ascontiguousarray(vec.reshape(-1, 128).T)

    ident = np.eye(128, dtype=f32)
    rotT = np.zeros((128, 128), f32)
    for d in range(16):
        rotT[d + 16, d] = -1.0
        rotT[d, d + 16] = 1.0
    cmat = np.stack([ident, rotT])
    inv_freq = (THETA ** (-np.arange(0, ROT, 2, dtype=np.float32) / ROT)).astype(f32)
    shared = {k: np.ascontiguousarray(np.asarray(inputs[k], f32)) for k in
              ("w_ada", "w_gate", "w_up", "w_down", "conv_w_pw1", "conv_w_pw2", "attn_w_q", "attn_w_k", "attn_w_v", "attn_w_o")}
    maps = []
    for (which, b, start) in _core_layout():
        if which == "p":
            xseq, cvec = xp[b], cp[b]
        else:
            xseq, cvec = xsm[b], cs[b]
        slen = xseq.shape[0]
        pos = start - BASEB * 128 + np.arange(NTOK)
        valid = (pos >= 0) & (pos < slen)
        x_ext = np.zeros((NTOK, D), f32)
        x_ext[valid] = xseq[pos[valid]]
        pvec = np.zeros((128, NPV), f32)
        pvec[:, PV_C:PV_C + 16] = fm(cvec)
        for l in range(4):
            o = PV_L + l * PV_LSZ
            pvec[:, o:o + 96] = fm(b_ada[l])
            for g in range(4):
                pvec[:, o + 96 + g * 16:o + 96 + (g + 1) * 16] = fm(norm_g[l, g])
        for jj in range(2):
            o = PV_CONV + jj * PV_CSZ
            for k in range(CW):
                pvec[:, o + k * 16:o + (k + 1) * 16] = fm(wdw[jj, k])
            pvec[:, o + 496:o + 512] = fm(bdw[jj])
            pvec[:, o + 512:o + 528] = fm(lng[jj])
            pvec[:, o + 528:o + 544] = fm(lnb[jj])
            o = PV_ATT + jj * PV_ASZ
            pvec[:, o:o + 16] = sink[jj][None, :]
        tokmask = np.broadcast_to(valid.astype(f32)[None, :], (128, NTOK)).copy()
        ang = pos.astype(f32)[:, None] * inv_freq[None, :]
        cosv = np.cos(ang).astype(f32)
        sinv = np.sin(ang).astype(f32)
        ropec = np.ones((128, NTOK), f32)
        ropes = np.zeros((128, NTOK), f32)
        ropec[0:16] = cosv.T
        ropec[16:32] = cosv.T
        ropes[0:16] = sinv.T
        ropes[16:32] = sinv.T
        amask = np.full((NBLK, 128, 384), NEGM, f32)
        ii = np.arange(128)[:, None]
        jj_ = np.arange(384)[None, :] - 128
        band = np.abs(ii - jj_) <= 128
        for bq in range(NBLK):
            kpos = pos[0] + bq * 128 + jj_[0]
            kv = (kpos >= 0) & (kpos < slen)
            amask[bq][band & kv[None, :]] = 0.0
        m = {"x_ext": x_ext, "pvec": pvec, "tokmask": tokmask, "ropec": ropec, "ropes": ropes, "amask": amask, "cmat": cmat}
        m.update(shared)
        maps.append(m)
    return maps


_PROG_CACHE = {}


def _run(inputs, depth=4):
    if depth not in _PROG_CACHE:
        _PROG_CACHE[depth] = Prog(depth).build()
    nc = _PROG_CACHE[depth]
    maps = _host_inputs(inputs, depth)
    res = run_bass_kernel_spmd(nc, maps, core_ids=list(range(NCORES)))
    outs = [np.asarray(r["y"], np.float32) for r in res.results]
    y_prompt = np.concatenate(outs[0:4], axis=0)[None]
    y_sample = np.stack([np.concatenate(outs[4:6], axis=0), np.concatenate(outs[6:8], axis=0)])
    return y_prompt, y_sample


def kernel(**inputs):
    return _run(inputs, depth=4)
```

```python
import os
import numpy as np
import ml_dtypes
import concourse.bass as bass
import concourse.mybir as mybir
from concourse.bass_utils import run_bass_kernel_spmd

F32 = mybir.dt.float32
DBG_SKIP = os.environ.get('DBG_SKIP', '')
BF16 = mybir.dt.bfloat16
AF = mybir.ActivationFunctionType
ALU = mybir.AluOpType
AX = mybir.AxisListType

D = 2048
NCH = 16
DFF = 5632
FCH = 44
NHEADS = 16
NKV = 4
CW = 31
EPS = 1e-6
ROT = 32
THETA = 500000.0
NCORES = 8
TOK_CORE = 4096
BASEB = 4
NBLK = 32 + 2 * BASEB
NTOK = NBLK * 128
SUBB = 3
NSUB = 2
NH = 2
WCAP = 8192
NWB = 3
ARENA_BYTES = 146 * 1024
SCALE = 128 ** -0.5
NEGM = -30000.0

PV_C = 0
PV_L = 16
PV_LSZ = 160
PV_CONV = PV_L + 4 * PV_LSZ
PV_CSZ = 544
PV_ATT = PV_CONV + 2 * PV_CSZ
PV_ASZ = 16
NPV = PV_ATT + 2 * PV_ASZ


class Buf:
    __slots__ = ("name", "w", "r", "multi", "ws", "excl")

    def __init__(self, name, multi=False, excl=False):
        self.name = name
        self.w = None
        self.r = {}
        self.multi = multi
        self.ws = {}
        self.excl = excl

    def wevents(self):
        if self.multi:
            return list(self.ws.items())
        return [self.w] if self.w is not None else []

    def setw(self, ev):
        if self.multi:
            self.ws[ev[0]] = max(self.ws.get(ev[0], 0), ev[1])
        else:
            self.w = ev
            self.r = {}


class Trk:
    def __init__(self, nc):
        self.nc = nc
        self.dry = False
        self.eng = {"pe": nc.tensor, "act": nc.scalar, "dve": nc.vector, "pool": nc.gpsimd, "sp": nc.sync}
        self.sems = {}
        self.cnt = {}
        self.seen = {k: {} for k in self.eng}
        self.ninstr = 0

    def semh(self, key):
        h = self.sems.get(key)
        if h is None:
            h = self.nc.alloc_semaphore("s_" + key)
            self.sems[key] = h
            self.cnt[key] = 0
        return h

    def _wait(self, e, deps):
        seen = self.seen[e]
        for (k, v) in deps:
            if k == e and e == "pe":
                continue
            if seen.get(k, 0) < v:
                self.eng[e].wait_ge(self.semh(k), v)
                seen[k] = v

    def op(self, e, fn, reads=(), writes=(), signal=True):
        if self.dry:
            return
        deps = []
        for b in reads:
            deps += b.wevents()
            if b.excl:
                deps += [(k, v) for (k, v) in b.r.items() if k != e]
        for b in writes:
            deps += b.wevents()
            deps += list(b.r.items())
        self._wait(e, deps)
        ins = fn()
        self.ninstr += 1
        self.semh(e)
        if signal:
            self.cnt[e] += 1
            ins.then_inc(self.sems[e], 1)
            ev = (e, self.cnt[e])
        else:
            ev = (e, self.cnt[e] + 1)
        for b in reads:
            if b.r.get(ev[0], 0) < ev[1]:
                b.r[ev[0]] = ev[1]
        for b in writes:
            b.setw(ev)

    def dma(self, q, pairs, reads, writes, sembuf):
        if self.dry:
            return
        key = "d_" + sembuf.name
        self.semh(key)
        deps = []
        for b in reads:
            deps += b.wevents()
        for b in writes:
            if not b.multi:
                deps += b.wevents()
            deps += list(b.r.items())
        if self.cnt[key]:
            deps.append((key, self.cnt[key]))
        self._wait(q, deps)
        for (o, i) in pairs:
            self.eng[q].dma_start(out=o, in_=i).then_inc(self.sems[key], 16)
            self.cnt[key] += 16
            self.ninstr += 1
        ev = (key, self.cnt[key])
        for b in reads:
            if b.r.get(key, 0) < ev[1]:
                b.r[key] = ev[1]
        for b in writes:
            b.setw(ev)

    def barrier(self, engines=("pe", "act", "dve", "pool", "sp")):
        if self.dry:
            return
        deps = [(k, v) for k, v in self.cnt.items() if v > 0]
        for e in engines:
            self._wait(e, deps)


def split_blocks(nb, unit):
    out = []
    while nb > 0:
        t = min(unit, nb)
        out.append(t)
        nb -= t
    return out


def supertiles(b0, b1):
    res = []
    b = b0
    while b < b1:
        nb = min(SUBB * NSUB, b1 - b)
        if nb == SUBB * NSUB:
            subs = [SUBB] * NSUB
        else:
            h = (nb + 1) // 2
            subs = [h, nb - h] if nb - h > 0 else [h]
        st = []
        bb = b
        for s in subs:
            st.append((bb * 128, s * 128))
            bb += s
        res.append(st)
        b += nb
    return res


def subtiles(b0, b1):
    return [s for st in supertiles(b0, b1) for s in st]


class Prog:
    def __init__(self, depth, dbg=None):
        self.depth = depth
        self.dbg = dbg
        nc = bass.Bass("TRN2", target_bir_lowering=False)
        self.nc = nc
        self.t = Trk(nc)
        dt = nc.dram_tensor
        self.x_in = dt("x_ext", [NTOK, D], F32, kind="ExternalInput").ap()
        self.pvec_in = dt("pvec", [128, NPV], F32, kind="ExternalInput").ap()
        self.tokmask_in = dt("tokmask", [128, NTOK], F32, kind="ExternalInput").ap()
        self.ropec_in = dt("ropec", [128, NTOK], F32, kind="ExternalInput").ap()
        self.ropes_in = dt("ropes", [128, NTOK], F32, kind="ExternalInput").ap()
        self.amask_in = dt("amask", [NBLK, 128, 384], F32, kind="ExternalInput").ap()
        self.cmat_in = dt("cmat", [2, 128, 128], F32, kind="ExternalInput").ap()
        if not dbg:
            self.w_ada = dt("w_ada", [4, D, 6 * D], F32, kind="ExternalInput").ap()
            self.w_gate = dt("w_gate", [4, D, DFF], F32, kind="ExternalInput").ap()
            self.w_up = dt("w_up", [4, D, DFF], F32, kind="ExternalInput").ap()
            self.w_down = dt("w_down", [4, DFF, D], F32, kind="ExternalInput").ap()
            self.w_pw1 = dt("conv_w_pw1", [2, D, 2 * D], F32, kind="ExternalInput").ap()
            self.w_pw2 = dt("conv_w_pw2", [2, D, D], F32, kind="ExternalInput").ap()
        self.w_q = dt("attn_w_q", [2, D, D], F32, kind="ExternalInput").ap()
        self.w_k = dt("attn_w_k", [2, D, 512], F32, kind="ExternalInput").ap()
        self.w_v = dt("attn_w_v", [2, D, 512], F32, kind="ExternalInput").ap()
        self.w_o = dt("attn_w_o", [2, D, D], F32, kind="ExternalInput").ap()
        self.y_out = dt("y", [TOK_CORE, D], F32, kind="ExternalOutput").ap()
        self.xs = dt("xs", [NCH, 128, NTOK], F32, kind=("ExternalOutput" if dbg else "Internal")).ap()
        self.us = dt("us", [NCH, 128, NTOK], F32, kind="Internal").ap()
        self.qs = dt("qs", [NCH, 128, NTOK], BF16, kind="Internal").ap()
        self.ks = dt("ks", [NKV, 128, NTOK], BF16, kind="Internal").ap()
        self.vs = dt("vs", [NTOK, 512], BF16, kind="Internal").ap()
        self.b_xs = Buf("xs", multi=True)
        self.b_us = Buf("us", multi=True)
        self.b_qs = Buf("qs", multi=True)
        self.b_ks = Buf("ks", multi=True)
        self.b_vs = Buf("vs", multi=True)
        self.b_in = Buf("inputs")
        al = nc.alloc_sbuf_tensor
        self.pvec = al("pvec_sb", [128, NPV], F32)
        self.b_pvec = Buf("pvec")
        self.identf = al("identf", [128, 128], F32)
        self.identb = al("identb", [128, 128], BF16)
        self.rotb = al("rotb", [128, 128], BF16)
        self.onesb = al("onesb", [128, 128], BF16)
        self.b_const = Buf("const", multi=True)
        self.cact = al("cact", [128, NCH], BF16)
        self.b_cact = Buf("cact")
        self.mod = al("mod", [128, 4, 96], F32)
        self.prm = al("prm", [128, 4, 4, NCH], F32)
        self.negsink = al("negsink", [128, 2, NHEADS], F32)
        self.b_mod = Buf("mod")
        self.wring = [al("wring%d" % i, [128, WCAP], BF16) for i in range(NWB)]
        self.b_wring = [Buf("wring%d" % i) for i in range(NWB)]
        self.arena = al("arena", [128, ARENA_BYTES // 2], BF16)
        self.arena_off = 0
        print("sbuf remaining after alloc", nc.sbuf_bytes_remaining)
        self.psum = [nc.alloc_psum_tensor("ps%d" % i, [128, 512], F32) for i in range(8)]
        self.b_psum = [Buf("ps%d" % i, excl=True) for i in range(8)]
        self.ps_i = 0
        self.nbuf = 0
        self.fills = []
        self.fill_i = 0
        self.fill_issued = 0

    def newbuf(self, name):
        self.nbuf += 1
        return Buf("%s_%d" % (name, self.nbuf))

    def arena_reset(self):
        self.arena_off = 0

    def carve(self, nbytes, dtype, shape):
        nbytes = (nbytes + 63) // 64 * 64
        off = self.arena_off
        assert off + nbytes <= ARENA_BYTES, ("arena overflow", off, nbytes)
        self.arena_off += nbytes
        ap = self.arena[:, off // 2:(off + nbytes) // 2]
        n = int(np.prod(shape))
        if dtype == F32:
            ap = ap.bitcast(F32)[:, 0:n]
        else:
            ap = ap[:, 0:n]
        if len(shape) == 2:
            ap = ap.rearrange("p (a b) -> p a b", a=shape[0])
        elif len(shape) == 3:
            ap = ap.rearrange("p (a b c) -> p a b c", a=shape[0], b=shape[1])
        return ap

    def cf32(self, shape):
        return self.carve(int(np.prod(shape)) * 4, F32, shape)

    def cbf(self, shape):
        return self.carve(int(np.prod(shape)) * 2, BF16, shape)

    def ps_next(self):
        i = self.ps_i
        self.ps_i = (i + 1) % 8
        return self.psum[i], self.b_psum[i]

    def pv(self, off, n):
        return self.pvec[:, off:off + n]

    def wfill_plan(self, parts):
        self.fills.append(parts)
        return len(self.fills) - 1

    def wissue(self):
        i = self.fill_issued
        if i >= len(self.fills):
            return
        self.fill_issued += 1
        slot = i % NWB
        wb = self.wring[slot]
        pairs = []
        for (off, kdim, ncols, src) in self.fills[i]:
            dst = wb[:, off:off + kdim * ncols].rearrange("p (k n) -> p k n", k=kdim)
            pairs.append((dst, src))
        self.t.dma("pool", pairs, reads=[self.b_in], writes=[self.b_wring[slot]], sembuf=self.b_wring[slot])

    def wnext(self, parts):
        if self.t.dry:
            self.wfill_plan(parts)
            return self.wring[0], self.b_wring[0]
        i = self.fill_i
        self.fill_i += 1
        assert i < self.fill_issued or True
        while self.fill_issued <= i:
            self.wissue()
        slot = i % NWB
        return self.wring[slot], self.b_wring[slot]

    def wrelease(self):
        if self.t.dry:
            return
        while self.fill_issued < min(len(self.fills), self.fill_i + NWB - 0):
            if self.fill_issued - NWB >= self.fill_i:
                break
            self.wissue()

    def rstd_from_ps(self, ss_ps, b_ss, n, rstd, b_rstd, tmp, b_tmp):
        t = self.t
        nc = self.nc
        t.op("dve", lambda: nc.vector.tensor_scalar(out=tmp[:, 0:n], in0=ss_ps[:, 0:n], scalar1=1.0 / D, scalar2=EPS,
                                                    op0=ALU.mult, op1=ALU.add), reads=[b_ss], writes=[b_tmp])
        t.op("dve", lambda: nc.vector.reciprocal(out=tmp[:, 0:n], in_=tmp[:, 0:n]), reads=[b_tmp], writes=[b_tmp])
        t.op("act", lambda: nc.scalar.activation(out=rstd[:, 0:n], in_=tmp[:, 0:n], func=AF.Sqrt), reads=[b_tmp], writes=[b_rstd])

    def sumsq(self, src, b_src, n, sq, b_sq, from_psum=False):
        t = self.t
        nc = self.nc
        ps, b_ps = self.ps_next()
        for k in range(NCH):
            j = k % 4
            t.op("act", lambda k=k, j=j: nc.scalar.activation(out=sq[:, j, 0:n], in_=src[:, k, 0:n], func=AF.Square),
                 reads=[b_src], writes=[b_sq[j]])
            t.op("pe", lambda k=k, j=j: nc.tensor.matmul(ps[:, 0:n], lhsT=self.onesb[:], rhs=sq[:, j, 0:n],
                                                        start=(k == 0), stop=(k == NCH - 1)),
                 reads=[b_sq[j], self.b_const], writes=([b_ps] if k == 0 else []), signal=True)
        b_ps.setw(("pe", t.cnt.get("pe", 0))) if not t.dry else None
        return ps, b_ps

    def prenorm(self, x, b_x, n, a_ap, sh_ap, h, b_h, scr):
        t = self.t
        nc = self.nc
        ps, b_ps = self.sumsq(x, b_x, n, scr["sq"], scr["b_sq"])
        self.rstd_from_ps(ps, b_ps, n, scr["rstd"], scr["b_rstd"], scr["tmp"], scr["b_tmp"])
        rstd = scr["rstd"]
        for k in range(NCH):
            j = k % 2
            tk = scr["tk"][:, j, 0:n]
            t.op("dve", lambda k=k, tk=tk: nc.vector.scalar_tensor_tensor(out=tk, in0=x[:, k, 0:n], scalar=a_ap[:, k:k + 1],
                                                                          in1=rstd[:, 0:n], op0=ALU.mult, op1=ALU.mult),
                 reads=[b_x, scr["b_rstd"], self.b_mod], writes=[scr["b_tk"][j]])
            t.op("dve", lambda k=k, tk=tk: nc.vector.tensor_scalar(out=h[:, k, 0:n], in0=tk, scalar1=sh_ap[:, k:k + 1], scalar2=None,
                                                                   op0=ALU.add),
                 reads=[scr["b_tk"][j], self.b_mod], writes=[b_h])

    def norm_scratch(self, nmax):
        s = {}
        s["sq"] = self.cbf([4, nmax])
        s["b_sq"] = [self.newbuf("sq") for _ in range(4)]
        s["rstd"] = self.cf32([nmax])[:, 0:nmax] if False else self.carve(nmax * 4, F32, [nmax])
        s["b_rstd"] = self.newbuf("rstd")
        s["tmp"] = self.carve(nmax * 4, F32, [nmax])
        s["b_tmp"] = self.newbuf("tmp")
        s["tk"] = self.cf32([2, nmax])
        s["b_tk"] = [self.newbuf("tk") for _ in range(2)]
        return s

    def load_x(self, x, b_x, tok0, n):
        self.t.dma("sp", [(x[:, :, 0:n], self.xs[:, :, tok0:tok0 + n].rearrange("c p t -> p c t"))],
                   reads=[self.b_xs], writes=[b_x], sembuf=b_x)

    def store_x(self, x, b_x, tok0, n):
        self.t.dma("sp", [(self.xs[:, :, tok0:tok0 + n].rearrange("c p t -> p c t"), x[:, :, 0:n])],
                   reads=[b_x], writes=[self.b_xs], sembuf=b_x)

    def phase_init(self):
        t = self.t
        nc = self.nc
        t.dma("sp", [(self.pvec[:], self.pvec_in)], reads=[self.b_in], writes=[self.b_pvec], sembuf=self.b_pvec)
        t.dma("sp", [(self.identf[:], self.cmat_in[0])], reads=[self.b_in], writes=[self.b_const], sembuf=self.b_const)
        t.dma("pool", [(self.identb[:], self.cmat_in[0]), (self.rotb[:], self.cmat_in[1])], reads=[self.b_in],
              writes=[self.b_const], sembuf=self.newbuf("constb"))
        t.op("dve", lambda: nc.vector.memset(self.onesb[:], 1.0), writes=[self.b_const])
        t.op("act", lambda: nc.scalar.activation(out=self.cact[:], in_=self.pv(PV_C, NCH), func=AF.Silu),
             reads=[self.b_pvec], writes=[self.b_cact])
        for j in range(2):
            t.op("dve", lambda j=j: nc.vector.tensor_scalar(out=self.negsink[:, j, :], in0=self.pv(PV_ATT + j * PV_ASZ, NHEADS),
                                                            scalar1=-1.0, scalar2=None, op0=ALU.mult),
                 reads=[self.b_pvec], writes=[self.b_mod])

    def phase_mod(self):
        t = self.t
        nc = self.nc
        for l in range(self.depth):
            ps, b_ps = self.ps_next()
            for og in range(24):
                src = self.w_ada[l][:, og * 512:(og + 1) * 512].rearrange("(k p) n -> p k n", p=128)
                wb, b_wb = self.wnext([(0, NCH, 512, src)])
                w3 = wb[:, 0:NCH * 512].rearrange("p (k n) -> p k n", k=NCH)
                for oc in range(4):
                    col = og * 4 + oc
                    for k in range(NCH):
                        first = (og == 0 and oc == 0 and k == 0)
                        t.op("pe", lambda oc=oc, k=k, col=col: nc.tensor.matmul(
                            ps[:, col:col + 1], lhsT=w3[:, k, oc * 128:(oc + 1) * 128], rhs=self.cact[:, k:k + 1],
                            start=(k == 0), stop=(k == NCH - 1)),
                            reads=[b_wb, self.b_cact], writes=([b_ps] if first else []),
                            signal=(k == NCH - 1 and oc == 3))
                self.wrelease()
            if not t.dry:
                b_ps.setw(("pe", t.cnt["pe"]))
            bada = self.pv(PV_L + l * PV_LSZ, 96)
            ng = self.pv(PV_L + l * PV_LSZ + 96, 64).rearrange("p (a b) -> p a b", a=4)
            md = self.mod[:, l, :]
            t.op("dve", lambda: nc.vector.tensor_tensor(out=md, in0=ps[:, 0:96], in1=bada, op=ALU.add),
                 reads=[b_ps, self.b_pvec], writes=[self.b_mod])
            for (dst, gi, mi, plus1) in ((0, 0, 1, True), (1, 1, 2, False), (2, 2, 4, True), (3, 3, 5, False)):
                if plus1:
                    t.op("dve", lambda dst=dst, gi=gi, mi=mi: nc.vector.scalar_tensor_tensor(
                        out=self.prm[:, l, dst, :], in0=md[:, mi * 16:(mi + 1) * 16], scalar=1.0, in1=ng[:, gi, :],
                        op0=ALU.add, op1=ALU.mult), reads=[self.b_mod, self.b_pvec], writes=[self.b_mod])
                else:
                    t.op("dve", lambda dst=dst, gi=gi, mi=mi: nc.vector.tensor_tensor(
                        out=self.prm[:, l, dst, :], in0=md[:, mi * 16:(mi + 1) * 16], in1=ng[:, gi, :], op=ALU.mult),
                        reads=[self.b_mod, self.b_pvec], writes=[self.b_mod])

    def phase_t0(self):
        t = self.t
        nc = self.nc
        self.arena_reset()
        xin = [self.cf32([D]) for _ in range(2)]
        b_xin = [self.newbuf("xin") for _ in range(2)]
        xo = [self.cf32([NCH, 512]) for _ in range(2)]
        b_xo = [self.newbuf("xo") for _ in range(2)]
        nblk_tot = NBLK
        for b in range(nblk_tot):
            s = b % 2
            g4 = (b // 4) % 2
            t.dma("sp", [(xin[s], self.x_in[b * 128:(b + 1) * 128, :])], reads=[self.b_in], writes=[b_xin[s]], sembuf=b_xin[s])
            for cg in range(4):
                ps, b_ps = self.ps_next()
                for j in range(4):
                    c = cg * 4 + j
                    t.op("pe", lambda c=c, j=j, ps=ps: nc.tensor.transpose(ps[:, j * 128:(j + 1) * 128], xin[s][:, c * 128:(c + 1) * 128],
                                                                            self.identf[:]),
                         reads=[b_xin[s], self.b_const], writes=([b_ps] if j == 0 else []), signal=(j == 3))
                if not t.dry:
                    b_ps.setw(("pe", t.cnt["pe"]))
                dst = xo[g4][:, cg * 4:(cg + 1) * 4, (b % 4) * 128:(b % 4) * 128 + 128]
                src = ps[:, :].rearrange("p (a b) -> p a b", a=4)
                if cg % 2 == 0:
                    t.op("dve", lambda dst=dst, src=src: nc.vector.tensor_copy(out=dst, in_=src), reads=[b_ps], writes=[b_xo[g4]])
                else:
                    t.op("act", lambda dst=dst, src=src: nc.scalar.copy(out=dst, in_=src), reads=[b_ps], writes=[b_xo[g4]])
            if b % 4 == 3:
                tok0 = (b - 3) * 128
                self.store_x(xo[g4], b_xo[g4], tok0, 512)
        t.barrier()

    def phase_c1(self, l, j):
        t = self.t
        nc = self.nc
        r = self.depth - 1 - l
        b0, b1 = BASEB - r - 1, NBLK - BASEB + r + 1
        self.arena_reset()
        nmax = SUBB * 128
        x = self.cf32([NCH, nmax])
        b_x = self.newbuf("x")
        h = [self.cbf([NCH, nmax]) for _ in range(NSUB)]
        b_h = [self.newbuf("h") for _ in range(NSUB)]
        u = [self.cf32([NCH, nmax]) for _ in range(NSUB)]
        b_u = [self.newbuf("u") for _ in range(NSUB)]
        tm = [self.carve(nmax * 4, F32, [nmax]) for _ in range(NSUB)]
        b_tm = [self.newbuf("tm") for _ in range(NSUB)]
        sig = self.cf32([3, nmax])
        b_sig = [self.newbuf("sig") for _ in range(3)]
        ut = self.cf32([3, nmax])
        b_ut = [self.newbuf("ut") for _ in range(3)]
        scr = self.norm_scratch(nmax)
        a_m = self.prm[:, l, 0, :]
        sh_m = self.mod[:, l, 0:16]
        w1 = self.w_pw1[j]
        si = 0
        for st in supertiles(b0, b1):
            for s, (tok0, n) in enumerate(st):
                self.load_x(x, b_x, tok0, n)
                t.dma("sp", [(tm[s][:, 0:n], self.tokmask_in[:, tok0:tok0 + n])], reads=[self.b_in], writes=[b_tm[s]], sembuf=b_tm[s])
                self.prenorm(x, b_x, n, a_m, sh_m, h[s], b_h[s], scr)
            for mg in range(8):
                srca = w1[:, mg * 256:(mg + 1) * 256].rearrange("(k p) n -> p k n", p=128)
                srcg = w1[:, D + mg * 256:D + (mg + 1) * 256].rearrange("(k p) n -> p k n", p=128)
                wb, b_wb = self.wnext([(0, NCH, 256, srca), (NCH * 256, NCH, 256, srcg)])
                wa = wb[:, 0:NCH * 256].rearrange("p (k n) -> p k n", k=NCH)
                wg = wb[:, NCH * 256:2 * NCH * 256].rearrange("p (k n) -> p k n", k=NCH)
                for mm in range(2):
                    m = mg * 2 + mm
                    for s, (tok0, n) in enumerate(st):
                        psa, b_psa = self.ps_next()
                        psg, b_psg = self.ps_next()
                        for (ps, b_ps, w3) in ((psg, b_psg, wg), (psa, b_psa, wa)):
                            for k in range(NCH):
                                t.op("pe", lambda ps=ps, w3=w3, k=k, s=s, n=n, mm=mm: nc.tensor.matmul(
                                    ps[:, 0:n], lhsT=w3[:, k, mm * 128:(mm + 1) * 128], rhs=h[s][:, k, 0:n],
                                    start=(k == 0), stop=(k == NCH - 1)),
                                    reads=[b_wb, b_h[s]], writes=([b_ps] if k == 0 else []), signal=(k == NCH - 1))
                            if not t.dry:
                                b_ps.setw(("pe", t.cnt["pe"]))
                        q = si % 3
                        si += 1
                        t.op("act", lambda q=q, n=n, psg=psg: nc.scalar.activation(out=sig[:, q, 0:n], in_=psg[:, 0:n], func=AF.Sigmoid),
                             reads=[b_psg], writes=[b_sig[q]])
                        t.op("dve", lambda q=q, n=n, psa=psa: nc.vector.tensor_tensor(out=ut[:, q, 0:n], in0=psa[:, 0:n], in1=sig[:, q, 0:n],
                                                                                      op=ALU.mult),
                             reads=[b_psa, b_sig[q]], writes=[b_ut[q]])
                        t.op("pool", lambda q=q, n=n, s=s, m=m: nc.gpsimd.tensor_tensor(out=u[s][:, m, 0:n], in0=ut[:, q, 0:n], in1=tm[s][:, 0:n],
                                                                                        op=ALU.mult),
                             reads=[b_ut[q], b_tm[s]], writes=[b_u[s]])
                self.wrelease()
            for s, (tok0, n) in enumerate(st):
                t.dma("sp", [(self.us[:, :, tok0:tok0 + n].rearrange("c p t -> p c t"), u[s][:, :, 0:n])],
                      reads=[b_u[s]], writes=[self.b_us], sembuf=b_u[s])
        t.barrier()

    def mix_residual(self, l, o, b_o, n, x, b_x, scr, tok0):
        t = self.t
        nc = self.nc
        ps, b_ps = self.sumsq(o, b_o, n, scr["sq"], scr["b_sq"])
        self.rstd_from_ps(ps, b_ps, n, scr["rstd"], scr["b_rstd"], scr["tmp"], scr["b_tmp"])
        gm = self.prm[:, l, 1, :]
        rstd = scr["rstd"]
        for k in range(NCH):
            t.op("dve", lambda k=k: nc.vector.scalar_tensor_tensor(out=o[:, k, 0:n], in0=o[:, k, 0:n], scalar=gm[:, k:k + 1],
                                                                   in1=rstd[:, 0:n], op0=ALU.mult, op1=ALU.mult),
                 reads=[b_o, scr["b_rstd"], self.b_mod], writes=[b_o])
            t.op("dve", lambda k=k: nc.vector.tensor_tensor(out=x[:, k, 0:n], in0=x[:, k, 0:n], in1=o[:, k, 0:n], op=ALU.add),
                 reads=[b_o, b_x], writes=[b_x])
        self.store_x(x, b_x, tok0, n)

    def phase_c2(self, l, j):
        t = self.t
        nc = self.nc
        r = self.depth - 1 - l
        b0, b1 = BASEB - r, NBLK - BASEB + r
        self.arena_reset()
        nmax = SUBB * 128
        HW = CW // 2
        uh = self.cf32([NCH, nmax + 2 * HW])
        b_uh = self.newbuf("uh")
        v = self.cf32([NCH, nmax])
        b_v = [self.newbuf("v") for _ in range(NCH)]
        z = self.cbf([NCH, nmax])
        b_z = self.newbuf("z")
        x = self.cf32([NCH, nmax])
        b_x = self.newbuf("x")
        vb = self.cbf([4, nmax])
        b_vb = [self.newbuf("vb") for _ in range(4)]
        mean = self.carve(nmax * 4, F32, [nmax])
        b_mean = self.newbuf("mean")
        msq = self.carve(nmax * 4, F32, [nmax])
        b_msq = self.newbuf("msq")
        scr = self.norm_scratch(nmax)
        pc = PV_CONV + j * PV_CSZ
        wdw = self.pv(pc, 496).rearrange("p (k c) -> p k c", k=CW)
        bdw = self.pv(pc + 496, 16)
        lng = self.pv(pc + 512, 16)
        lnb = self.pv(pc + 528, 16)
        w2 = self.w_pw2[j]
        o = v
        for (tok0, n) in subtiles(b0, b1):
            t.dma("sp", [(uh[:, :, 0:n + 2 * HW], self.us[:, :, tok0 - HW:tok0 + n + HW].rearrange("c p t -> p c t"))],
                  reads=[self.b_us], writes=[b_uh], sembuf=b_uh)
            self.load_x(x, b_x, tok0, n)
            for k in range(CW):
                for c in range(NCH):
                    if k == 0:
                        t.op("dve", lambda c=c: nc.vector.tensor_scalar(out=v[:, c, 0:n], in0=uh[:, c, 0:n], scalar1=wdw[:, 0, c:c + 1],
                                                                        scalar2=bdw[:, c:c + 1], op0=ALU.mult, op1=ALU.add),
                             reads=[b_uh, self.b_pvec], writes=[b_v[c]])
                    else:
                        t.op("dve", lambda c=c, k=k: nc.vector.scalar_tensor_tensor(out=v[:, c, 0:n], in0=uh[:, c, k:k + n],
                                                                                    scalar=wdw[:, k, c:c + 1], in1=v[:, c, 0:n],
                                                                                    op0=ALU.mult, op1=ALU.add),
                             reads=[b_uh, b_v[c]], writes=[b_v[c]])
            psm, b_psm = self.ps_next()
            pss, b_pss = self.ps_next()
            for c in range(NCH):
                q = c % 4
                t.op("dve", lambda c=c, q=q: nc.vector.tensor_copy(out=vb[:, q, 0:n], in_=v[:, c, 0:n]), reads=[b_v[c]], writes=[b_vb[q]])
                t.op("pe", lambda c=c, q=q: nc.tensor.matmul(psm[:, 0:n], lhsT=self.onesb[:], rhs=vb[:, q, 0:n], start=(c == 0), stop=(c == NCH - 1)),
                     reads=[b_vb[q], self.b_const], writes=([b_psm] if c == 0 else []), signal=True)
                q2 = c % 4
                t.op("act", lambda c=c, q2=q2: nc.scalar.activation(out=scr["sq"][:, q2, 0:n], in_=v[:, c, 0:n], func=AF.Square),
                     reads=[b_v[c]], writes=[scr["b_sq"][q2]])
                t.op("pe", lambda c=c, q2=q2: nc.tensor.matmul(pss[:, 0:n], lhsT=self.onesb[:], rhs=scr["sq"][:, q2, 0:n], start=(c == 0),
                                                             stop=(c == NCH - 1)),
                     reads=[scr["b_sq"][q2], self.b_const], writes=([b_pss] if c == 0 else []), signal=True)
            if not t.dry:
                b_psm.setw(("pe", t.cnt["pe"]))
                b_pss.setw(("pe", t.cnt["pe"]))
            t.op("dve", lambda: nc.vector.tensor_scalar(out=mean[:, 0:n], in0=psm[:, 0:n], scalar1=1.0 / D, scalar2=None, op0=ALU.mult),
                 reads=[b_psm], writes=[b_mean])
            t.op("dve", lambda: nc.vector.tensor_tensor(out=msq[:, 0:n], in0=mean[:, 0:n], in1=mean[:, 0:n], op=ALU.mult),
                 reads=[b_mean], writes=[b_msq])
            tmp = scr["tmp"]
            t.op("dve", lambda: nc.vector.scalar_tensor_tensor(out=tmp[:, 0:n], in0=pss[:, 0:n], scalar=1.0 / D, in1=msq[:, 0:n],
                                                               op0=ALU.mult, op1=ALU.subtract),
                 reads=[b_pss, b_msq], writes=[scr["b_tmp"]])
            t.op("dve", lambda: nc.vector.tensor_scalar(out=tmp[:, 0:n], in0=tmp[:, 0:n], scalar1=EPS, scalar2=None, op0=ALU.add),
                 reads=[scr["b_tmp"]], writes=[scr["b_tmp"]])
            t.op("dve", lambda: nc.vector.reciprocal(out=tmp[:, 0:n], in_=tmp[:, 0:n]), reads=[scr["b_tmp"]], writes=[scr["b_tmp"]])
            rstd = scr["rstd"]
            t.op("act", lambda: nc.scalar.activation(out=rstd[:, 0:n], in_=tmp[:, 0:n], func=AF.Sqrt), reads=[scr["b_tmp"]],
                 writes=[scr["b_rstd"]])
            for c in range(NCH):
                t.op("dve", lambda c=c: nc.vector.tensor_tensor(out=v[:, c, 0:n], in0=v[:, c, 0:n], in1=mean[:, 0:n], op=ALU.subtract),
                     reads=[b_v[c], b_mean], writes=[b_v[c]])
                t.op("dve", lambda c=c: nc.vector.tensor_tensor(out=v[:, c, 0:n], in0=v[:, c, 0:n], in1=rstd[:, 0:n], op=ALU.mult),
                     reads=[b_v[c], scr["b_rstd"]], writes=[b_v[c]])
                t.op("act", lambda c=c: nc.scalar.activation(out=z[:, c, 0:n], in_=v[:, c, 0:n], func=AF.Silu, bias=lnb[:, c:c + 1],
                                                             scale=lng[:, c:c + 1]),
                     reads=[b_v[c], self.b_pvec], writes=[b_z])
            for mg in range(4):
                src = w2[:, mg * 512:(mg + 1) * 512].rearrange("(k p) n -> p k n", p=128)
                wb, b_wb = self.wnext([(0, NCH, 512, src)])
                w3 = wb[:, 0:NCH * 512].rearrange("p (k n) -> p k n", k=NCH)
                for mm in range(4):
                    m = mg * 4 + mm
                    ps, b_ps = self.ps_next()
                    for k in range(NCH):
                        t.op("pe", lambda ps=ps, k=k, mm=mm: nc.tensor.matmul(ps[:, 0:n], lhsT=w3[:, k, mm * 128:(mm + 1) * 128], rhs=z[:, k, 0:n],
                                                                              start=(k == 0), stop=(k == NCH - 1)),
                             reads=[b_wb, b_z], writes=([b_ps] if k == 0 else []), signal=(k == NCH - 1))
                    if not t.dry:
                        b_ps.setw(("pe", t.cnt["pe"]))
                    if m % 2 == 0:
                        t.op("dve", lambda ps=ps, m=m: nc.vector.tensor_copy(out=o[:, m, 0:n], in_=ps[:, 0:n]), reads=[b_ps], writes=[b_v[m]])
                    else:
                        t.op("act", lambda ps=ps, m=m: nc.scalar.copy(out=o[:, m, 0:n], in_=ps[:, 0:n]), reads=[b_ps], writes=[b_v[m]])
                self.wrelease()
            b_o = self.newbuf("o")
            if not t.dry:
                b_o.multi = True
                for c in range(NCH):
                    for ev in b_v[c].wevents():
                        b_o.setw(ev)
            self.mix_residual_multi(l, o, b_o, b_v, n, x, b_x, scr, tok0)
        t.barrier()

    def mix_residual_multi(self, l, o, b_o, b_oc, n, x, b_x, scr, tok0):
        t = self.t
        nc = self.nc
        ps, b_ps = self.ps_next()
        sq, b_sq = scr["sq"], scr["b_sq"]
        for k in range(NCH):
            jq = k % 4
            t.op("act", lambda k=k, jq=jq: nc.scalar.activation(out=sq[:, jq, 0:n], in_=o[:, k, 0:n], func=AF.Square),
                 reads=[b_oc[k]], writes=[b_sq[jq]])
            t.op("pe", lambda k=k, jq=jq: nc.tensor.matmul(ps[:, 0:n], lhsT=self.onesb[:], rhs=sq[:, jq, 0:n], start=(k == 0), stop=(k == NCH - 1)),
                 reads=[b_sq[jq], self.b_const], writes=([b_ps] if k == 0 else []), signal=True)
        if not t.dry:
            b_ps.setw(("pe", t.cnt["pe"]))
        self.rstd_from_ps(ps, b_ps, n, scr["rstd"], scr["b_rstd"], scr["tmp"], scr["b_tmp"])
        gm = self.prm[:, l, 1, :]
        rstd = scr["rstd"]
        for k in range(NCH):
            t.op("dve", lambda k=k: nc.vector.scalar_tensor_tensor(out=o[:, k, 0:n], in0=o[:, k, 0:n], scalar=gm[:, k:k + 1],
                                                                   in1=rstd[:, 0:n], op0=ALU.mult, op1=ALU.mult),
                 reads=[b_oc[k], scr["b_rstd"], self.b_mod], writes=[b_oc[k]])
        for k in range(NCH):
            t.op("dve", lambda k=k: nc.vector.tensor_tensor(out=x[:, k, 0:n], in0=x[:, k, 0:n], in1=o[:, k, 0:n], op=ALU.add),
                 reads=[b_oc[k], b_x], writes=[b_x])
        self.store_x(x, b_x, tok0, n)

    def phase_ffn(self, l, last):
        t = self.t
        nc = self.nc
        r = self.depth - 1 - l
        b0, b1 = BASEB - r, NBLK - BASEB + r
        self.arena_reset()
        nmax = SUBB * 128
        FH = FCH // NH
        h2 = [self.cbf([NCH, nmax]) for _ in range(NSUB)]
        b_h2 = [self.newbuf("h2") for _ in range(NSUB)]
        y = [self.cf32([NCH, nmax]) for _ in range(NSUB)]
        b_y = [[self.newbuf("y") for _ in range(NCH)] for _ in range(NSUB)]
        act = [self.cbf([FH, nmax]) for _ in range(NSUB)]
        b_act = [[self.newbuf("act") for _ in range(FH)] for _ in range(NSUB)]
        x = self.cf32([NCH, nmax])
        b_x = self.newbuf("x")
        sg = self.cf32([3, nmax])
        b_sg = [self.newbuf("sg") for _ in range(3)]
        scr = self.norm_scratch(nmax)
        if last:
            yof = act[0].rearrange("p a b -> p (a b)").bitcast(F32)
            yo = [yof[:, 0:D], yof[:, D:2 * D]]
            b_yo = [self.newbuf("yo") for _ in range(2)]
        a_f = self.prm[:, l, 2, :]
        sh_f = self.mod[:, l, 48:64]
        gf = self.prm[:, l, 3, :]
        wg_d, wu_d, wd_d = self.w_gate[l], self.w_up[l], self.w_down[l]
        si = 0
        yoi = 0
        for st in supertiles(b0, b1):
            for s, (tok0, n) in enumerate(st):
                self.load_x(x, b_x, tok0, n)
                self.prenorm(x, b_x, n, a_f, sh_f, h2[s], b_h2[s], scr)
            for hf in range(NH):
                mlist = list(range(hf * FH, (hf + 1) * FH))
                for g0 in range(0, FH, 2):
                    ms = mlist[g0:g0 + 2]
                    nm = len(ms)
                    c0 = ms[0] * 128
                    srcg = wg_d[:, c0:c0 + nm * 128].rearrange("(k p) n -> p k n", p=128)
                    srcu = wu_d[:, c0:c0 + nm * 128].rearrange("(k p) n -> p k n", p=128)
                    wb, b_wb = self.wnext([(0, NCH, nm * 128, srcg), (NCH * 256, NCH, nm * 128, srcu)])
                    wg3 = wb[:, 0:NCH * nm * 128].rearrange("p (k n) -> p k n", k=NCH)
                    wu3 = wb[:, NCH * 256:NCH * 256 + NCH * nm * 128].rearrange("p (k n) -> p k n", k=NCH)
                    for mi, m in enumerate(ms):
                        ml = m - hf * FH
                        for s, (tok0, n) in enumerate(st):
                            psg, b_psg = self.ps_next()
                            psu, b_psu = self.ps_next()
                            for (ps, b_ps, w3) in ((psg, b_psg, wg3), (psu, b_psu, wu3)):
                                for k in range(NCH):
                                    t.op("pe", lambda ps=ps, w3=w3, k=k, s=s, n=n, mi=mi: nc.tensor.matmul(
                                        ps[:, 0:n], lhsT=w3[:, k, mi * 128:(mi + 1) * 128], rhs=h2[s][:, k, 0:n],
                                        start=(k == 0), stop=(k == NCH - 1)),
                                        reads=[b_wb, b_h2[s]], writes=([b_ps] if k == 0 else []), signal=(k == NCH - 1))
                                if not t.dry:
                                    b_ps.setw(("pe", t.cnt["pe"]))
                            q = si % 3
                            si += 1
                            t.op("act", lambda q=q, n=n, psg=psg: nc.scalar.activation(out=sg[:, q, 0:n], in_=psg[:, 0:n], func=AF.Silu),
                                 reads=[b_psg], writes=[b_sg[q]])
                            t.op("dve", lambda q=q, n=n, psu=psu, s=s, ml=ml: nc.vector.tensor_tensor(
                                out=act[s][:, ml, 0:n], in0=psu[:, 0:n], in1=sg[:, q, 0:n], op=ALU.mult),
                                reads=[b_psu, b_sg[q]], writes=[b_act[s][ml]])
                    self.wrelease()
                for og in range(NCH // 2):
                    src = wd_d[hf * FH * 128:(hf + 1) * FH * 128, og * 256:(og + 1) * 256].rearrange("(k p) n -> p k n", p=128)
                    wb, b_wb = self.wnext([(0, FH, 256, src)])
                    wd3 = wb[:, 0:FH * 256].rearrange("p (k n) -> p k n", k=FH)
                    for oc in range(2):
                        m = og * 2 + oc
                        for s, (tok0, n) in enumerate(st):
                            ps, b_ps = self.ps_next()
                            for k in range(FH):
                                t.op("pe", lambda ps=ps, k=k, s=s, n=n, oc=oc: nc.tensor.matmul(
                                    ps[:, 0:n], lhsT=wd3[:, k, oc * 128:(oc + 1) * 128], rhs=act[s][:, k, 0:n],
                                    start=(k == 0), stop=(k == FH - 1)),
                                    reads=[b_wb, b_act[s][k]], writes=([b_ps] if k == 0 else []), signal=(k == FH - 1))
                            if not t.dry:
                                b_ps.setw(("pe", t.cnt["pe"]))
                            if hf == 0:
                                t.op("act", lambda ps=ps, s=s, m=m, n=n: nc.scalar.copy(out=y[s][:, m, 0:n], in_=ps[:, 0:n]),
                                     reads=[b_ps], writes=[b_y[s][m]])
                            else:
                                t.op("dve", lambda ps=ps, s=s, m=m, n=n: nc.vector.tensor_tensor(out=y[s][:, m, 0:n], in0=ps[:, 0:n],
                                                                                               in1=y[s][:, m, 0:n], op=ALU.add),
                                     reads=[b_ps, b_y[s][m]], writes=[b_y[s][m]])
                    self.wrelease()
            for s, (tok0, n) in enumerate(st):
                ps, b_ps = self.ps_next()
                sq, b_sq = scr["sq"], scr["b_sq"]
                for k in range(NCH):
                    jq = k % 4
                    t.op("act", lambda k=k, jq=jq, s=s, n=n: nc.scalar.activation(out=sq[:, jq, 0:n], in_=y[s][:, k, 0:n], func=AF.Square),
                         reads=[b_y[s][k]], writes=[b_sq[jq]])
                    t.op("pe", lambda k=k, jq=jq, n=n, ps=ps: nc.tensor.matmul(ps[:, 0:n], lhsT=self.onesb[:], rhs=sq[:, jq, 0:n], start=(k == 0),
                                                                             stop=(k == NCH - 1)),
                         reads=[b_sq[jq], self.b_const], writes=([b_ps] if k == 0 else []), signal=True)
                if not t.dry:
                    b_ps.setw(("pe", t.cnt["pe"]))
                self.rstd_from_ps(ps, b_ps, n, scr["rstd"], scr["b_rstd"], scr["tmp"], scr["b_tmp"])
                rstd = scr["rstd"]
                self.load_x(x, b_x, tok0, n)
                for k in range(NCH):
                    t.op("dve", lambda k=k, s=s, n=n: nc.vector.scalar_tensor_tensor(out=y[s][:, k, 0:n], in0=y[s][:, k, 0:n], scalar=gf[:, k:k + 1],
                                                                                   in1=rstd[:, 0:n], op0=ALU.mult, op1=ALU.mult),
                         reads=[b_y[s][k], scr["b_rstd"], self.b_mod], writes=[b_y[s][k]])
                for k in range(NCH):
                    t.op("dve", lambda k=k, s=s, n=n: nc.vector.tensor_tensor(out=x[:, k, 0:n], in0=x[:, k, 0:n], in1=y[s][:, k, 0:n], op=ALU.add),
                         reads=[b_y[s][k], b_x], writes=[b_x])
                if not last:
                    self.store_x(x, b_x, tok0, n)
                else:
                    for bb in range(n // 128):
                        tokb = tok0 + bb * 128 - BASEB * 128
                        qy = yoi % 2
                        yoi += 1
                        for cg in range(4):
                            ps, b_ps = self.ps_next()
                            for jj in range(4):
                                c = cg * 4 + jj
                                t.op("pe", lambda c=c, jj=jj, ps=ps, bb=bb: nc.tensor.transpose(
                                    ps[:, jj * 128:(jj + 1) * 128], x[:, c, bb * 128:(bb + 1) * 128], self.identf[:]),
                                    reads=[b_x, self.b_const], writes=([b_ps] if jj == 0 else []), signal=(jj == 3))
                            if not t.dry:
                                b_ps.setw(("pe", t.cnt["pe"]))
                            if cg % 2 == 0:
                                t.op("dve", lambda ps=ps, cg=cg, qy=qy: nc.vector.tensor_copy(out=yo[qy][:, cg * 512:(cg + 1) * 512], in_=ps[:, :]),
                                     reads=[b_ps], writes=[b_yo[qy]])
                            else:
                                t.op("act", lambda ps=ps, cg=cg, qy=qy: nc.scalar.copy(out=yo[qy][:, cg * 512:(cg + 1) * 512], in_=ps[:, :]),
                                     reads=[b_ps], writes=[b_yo[qy]])
                        t.dma("sp", [(self.y_out[tokb:tokb + 128, :], yo[qy])], reads=[b_yo[qy]] + b_act[0], writes=[], sembuf=b_yo[qy])
        t.barrier()

    def phase_a1(self, l, j, rng=None):
        t = self.t
        nc = self.nc
        r = self.depth - 1 - l
        b0, b1 = BASEB - r - 1, NBLK - BASEB + r + 1
        if rng:
            b0, b1 = rng
        self.arena_reset()
        nmax = SUBB * 128
        x = self.cf32([NCH, nmax])
        b_x = self.newbuf("x")
        h = [self.cbf([NCH, nmax]) for _ in range(NSUB)]
        b_h = [self.newbuf("h") for _ in range(NSUB)]
        qT = [self.cbf([NCH, nmax]) for _ in range(NSUB)]
        b_qT = [self.newbuf("qT") for _ in range(NSUB)]
        kT = [self.cbf([NKV, nmax]) for _ in range(NSUB)]
        b_kT = [self.newbuf("kT") for _ in range(NSUB)]
        vt = [self.cbf([SUBB, 512]) for _ in range(NSUB)]
        b_vt = [self.newbuf("vt") for _ in range(NSUB)]
        rc = [self.carve(nmax * 4, F32, [nmax]) for _ in range(NSUB)]
        rs = [self.carve(nmax * 4, F32, [nmax]) for _ in range(NSUB)]
        b_rcs = [self.newbuf("rcs") for _ in range(NSUB)]
        qb = self.cbf([2, nmax])
        b_qb = [self.newbuf("qb") for _ in range(2)]
        t1 = self.cf32([2, nmax])
        b_t1 = [self.newbuf("t1") for _ in range(2)]
        t2 = self.cf32([2, nmax])
        b_t2 = [self.newbuf("t2") for _ in range(2)]
        scr = self.norm_scratch(nmax)
        a_m = self.prm[:, l, 0, :]
        sh_m = self.mod[:, l, 0:16]
        ri = 0
        for st in supertiles(b0, b1):
            for s, (tok0, n) in enumerate(st):
                self.load_x(x, b_x, tok0, n)
                t.dma("sp", [(rc[s][:, 0:n], self.ropec_in[:, tok0:tok0 + n]), (rs[s][:, 0:n], self.ropes_in[:, tok0:tok0 + n])],
                      reads=[self.b_in], writes=[b_rcs[s]], sembuf=b_rcs[s])
                self.prenorm(x, b_x, n, a_m, sh_m, h[s], b_h[s], scr)
            for fi in range(5):
                if fi < 4:
                    src = self.w_q[j][:, fi * 512:(fi + 1) * 512].rearrange("(k p) n -> p k n", p=128)
                else:
                    src = self.w_k[j].rearrange("(k p) n -> p k n", p=128)
                wb, b_wb = self.wnext([(0, NCH, 512, src)])
                w3 = wb[:, 0:NCH * 512].rearrange("p (k n) -> p k n", k=NCH)
                for mm in range(4):
                    for s, (tok0, n) in enumerate(st):
                        if fi < 4:
                            dst, b_dst = qT[s][:, fi * 4 + mm, 0:n], b_qT[s]
                        else:
                            dst, b_dst = kT[s][:, mm, 0:n], b_kT[s]
                        ps, b_ps = self.ps_next()
                        for k in range(NCH):
                            t.op("pe", lambda ps=ps, k=k, s=s, n=n, mm=mm: nc.tensor.matmul(
                                ps[:, 0:n], lhsT=w3[:, k, mm * 128:(mm + 1) * 128], rhs=h[s][:, k, 0:n],
                                start=(k == 0), stop=(k == NCH - 1)),
                                reads=[b_wb, b_h[s]], writes=([b_ps] if k == 0 else []), signal=(k == NCH - 1))
                        if not t.dry:
                            b_ps.setw(("pe", t.cnt["pe"]))
                        q = ri % 2
                        ri += 1
                        if "rope" in DBG_SKIP:
                            t.op("act", lambda ps=ps, dst=dst: nc.scalar.copy(out=dst, in_=ps[:, 0:n]), reads=[b_ps], writes=[b_dst])
                            continue
                        t.op("act", lambda ps=ps, q=q, n=n: nc.scalar.copy(out=qb[:, q, 0:n], in_=ps[:, 0:n]), reads=[b_ps], writes=[b_qb[q]])
                        if "rotmm" in DBG_SKIP:
                            ps2, b_ps2 = ps, b_ps
                        else:
                            ps2, b_ps2 = self.ps_next()
                            t.op("pe", lambda ps2=ps2, q=q, n=n: nc.tensor.matmul(ps2[:, 0:n], lhsT=self.rotb[:], rhs=qb[:, q, 0:n], start=True, stop=True),
                                 reads=[b_qb[q], self.b_const], writes=[b_ps2])
                        t.op("dve", lambda ps=ps, q=q, n=n, s=s: nc.vector.tensor_tensor(out=t1[:, q, 0:n], in0=ps[:, 0:n], in1=rc[s][:, 0:n], op=ALU.mult),
                             reads=[b_ps, b_rcs[s]], writes=[b_t1[q]])
                        t.op("dve", lambda ps2=ps2, q=q, n=n, s=s: nc.vector.tensor_tensor(out=t2[:, q, 0:n], in0=ps2[:, 0:n], in1=rs[s][:, 0:n], op=ALU.mult),
                             reads=[b_ps2, b_rcs[s]], writes=[b_t2[q]])
                        t.op("dve", lambda dst=dst, q=q, n=n: nc.vector.tensor_tensor(out=dst, in0=t1[:, q, 0:n], in1=t2[:, q, 0:n], op=ALU.add),
                             reads=[b_t1[q], b_t2[q]], writes=[b_dst])
                self.wrelease()
            src = self.w_v[j].rearrange("(k p) n -> p k n", p=128)
            wb, b_wb = self.wnext([(0, NCH, 512, src)])
            w3 = wb[:, 0:NCH * 512].rearrange("p (k n) -> p k n", k=NCH)
            for s, (tok0, n) in enumerate(st):
                for bb in range(n // 128 if "v" not in DBG_SKIP.split(",") else 0):
                    ps, b_ps = self.ps_next()
                    for k in range(NCH):
                        t.op("pe", lambda ps=ps, k=k, s=s, bb=bb: nc.tensor.matmul(
                            ps[:, :], lhsT=h[s][:, k, bb * 128:(bb + 1) * 128], rhs=w3[:, k, :], start=(k == 0), stop=(k == NCH - 1)),
                            reads=[b_wb, b_h[s]], writes=([b_ps] if k == 0 else []), signal=(k == NCH - 1))
                    if not t.dry:
                        b_ps.setw(("pe", t.cnt["pe"]))
                    if bb % 2 == 0:
                        t.op("dve", lambda ps=ps, s=s, bb=bb: nc.vector.tensor_copy(out=vt[s][:, bb, :], in_=ps[:, :]), reads=[b_ps], writes=[b_vt[s]])
                    else:
                        t.op("act", lambda ps=ps, s=s, bb=bb: nc.scalar.copy(out=vt[s][:, bb, :], in_=ps[:, :]), reads=[b_ps], writes=[b_vt[s]])
            self.wrelease()
            for s, (tok0, n) in enumerate(st if "st" not in DBG_SKIP.split(",") else []):
                nb = n // 128
                t.dma("sp", [(self.qs[:, :, tok0:tok0 + n].rearrange("c p t -> p c t"), qT[s][:, :, 0:n])],
                      reads=[b_qT[s]], writes=[self.b_qs], sembuf=b_qT[s])
                t.dma("sp", [(self.ks[:, :, tok0:tok0 + n].rearrange("c p t -> p c t"), kT[s][:, :, 0:n])],
                      reads=[b_kT[s]], writes=[self.b_ks], sembuf=b_kT[s])
                t.dma("sp", [(self.vs[tok0:tok0 + n, :].rearrange("(b p) f -> p b f", p=128), vt[s][:, 0:nb, :])],
                      reads=[b_vt[s]], writes=[self.b_vs], sembuf=b_vt[s])
        t.barrier()

    def phase_a2(self, l, j, rng=None):
        t = self.t
        nc = self.nc
        r = self.depth - 1 - l
        b0, b1 = BASEB - r, NBLK - BASEB + r
        if rng:
            b0, b1 = rng
        self.arena_reset()
        nmax = SUBB * 128
        qT = self.cbf([NCH, nmax])
        b_qT = self.newbuf("qT")
        kT = self.cbf([NKV, nmax + 256])
        b_kT = self.newbuf("kT")
        vt = self.cbf([SUBB + 2, 512])
        b_vt = self.newbuf("vt")
        mk = self.cbf([SUBB, 384])
        b_mk = self.newbuf("mk")
        oT = self.cbf([NCH, nmax])
        b_oT = self.newbuf("oT")
        ao = self.cf32([NCH, nmax])
        b_ao = [self.newbuf("ao") for _ in range(NCH)]
        x = self.cf32([NCH, nmax])
        b_x = self.newbuf("x")
        p = self.cbf([4, 384])
        b_p = [self.newbuf("p") for _ in range(4)]
        pT = self.cbf([4, 384])
        b_pT = [self.newbuf("pT") for _ in range(4)]
        dg = self.cbf([4, 128])
        b_dg = [self.newbuf("dg") for _ in range(4)]
        sm = self.cf32([2, 6, 4])
        b_sm = [self.newbuf("sm") for _ in range(2)]
        scr = self.norm_scratch(nmax)
        wo = self.w_o[j]
        negsink = self.negsink[:, j, :]
        sink = self.pv(PV_ATT + j * PV_ASZ, NHEADS)
        gi = 0
        pi = 0
        for (tok0, n) in subtiles(b0, b1):
            nb = n // 128
            blk0 = tok0 // 128
            t.dma("sp", [(qT[:, :, 0:n], self.qs[:, :, tok0:tok0 + n].rearrange("c p t -> p c t"))],
                  reads=[self.b_qs], writes=[b_qT], sembuf=b_qT)
            t.dma("sp", [(kT[:, :, 0:n + 256], self.ks[:, :, tok0 - 128:tok0 + n + 128].rearrange("c p t -> p c t"))],
                  reads=[self.b_ks], writes=[b_kT], sembuf=b_kT)
            t.dma("sp", [(vt[:, 0:nb + 2, :], self.vs[tok0 - 128:tok0 + n + 128, :].rearrange("(b p) f -> p b f", p=128))],
                  reads=[self.b_vs], writes=[b_vt], sembuf=b_vt)
            t.dma("pool", [(mk[:, 0:nb, :], self.amask_in[blk0:blk0 + nb].rearrange("b p j -> p b j"))],
                  reads=[self.b_in], writes=[b_mk], sembuf=b_mk)
            self.load_x(x, b_x, tok0, n)
            for qbk in range(nb):
                for kvh in range(NKV):
                    g = gi % 2
                    gi += 1
                    smg = sm[:, g, :, :]
                    scs = []
                    for hh in range(4):
                        hd = kvh * 4 + hh
                        ps, b_ps = self.ps_next()
                        scs.append((ps, b_ps))
                        t.op("pe", lambda ps=ps, hd=hd, qbk=qbk, kvh=kvh: nc.tensor.matmul(
                            ps[:, 0:384], lhsT=qT[:, hd, qbk * 128:(qbk + 1) * 128], rhs=kT[:, kvh, qbk * 128:qbk * 128 + 384],
                            start=True, stop=False), reads=[b_qT, b_kT], writes=[b_ps], signal=False)
                        t.op("pe", lambda ps=ps, qbk=qbk: nc.tensor.matmul(ps[:, 0:384], lhsT=self.identb[:], rhs=mk[:, qbk, :], start=False, stop=True),
                             reads=[b_mk, self.b_const], writes=[], signal=True)
                        if not t.dry:
                            b_ps.setw(("pe", t.cnt["pe"]))
                        t.op("dve", lambda ps=ps, hh=hh, smg=smg: nc.vector.reduce_max(out=smg[:, 0, hh:hh + 1], in_=ps[:, 0:384], axis=AX.X),
                             reads=[b_ps], writes=[b_sm[g]])
                    t.op("dve", lambda smg=smg: nc.vector.tensor_scalar(out=smg[:, 1, :], in0=smg[:, 0, :], scalar1=-SCALE, scalar2=None, op0=ALU.mult),
                         reads=[b_sm[g]], writes=[b_sm[g]])
                    t.op("dve", lambda smg=smg, kvh=kvh: nc.vector.tensor_tensor(out=smg[:, 1, :], in0=smg[:, 1, :], in1=negsink[:, kvh * 4:kvh * 4 + 4],
                                                                                 op=ALU.min),
                         reads=[b_sm[g], self.b_mod], writes=[b_sm[g]])
                    pis = []
                    for hh in range(4):
                        hd = kvh * 4 + hh
                        ps, b_ps = scs[hh]
                        q = pi % 4
                        pi += 1
                        pis.append(q)
                        t.op("act", lambda ps=ps, q=q, hh=hh, smg=smg: nc.scalar.activation(
                            out=p[:, q, :], in_=ps[:, 0:384], func=AF.Exp, bias=smg[:, 1, hh:hh + 1], scale=SCALE,
                            accum_out=smg[:, 2, hh:hh + 1]), reads=[b_ps, b_sm[g]], writes=[b_p[q], b_sm[g]])
                        t.op("act", lambda hh=hh, hd=hd, smg=smg: nc.scalar.activation(
                            out=smg[:, 3, hh:hh + 1], in_=smg[:, 1, hh:hh + 1], func=AF.Exp, bias=sink[:, hd:hd + 1], scale=1.0),
                            reads=[b_sm[g], self.b_pvec], writes=[b_sm[g]])
                    t.op("dve", lambda smg=smg: nc.vector.tensor_tensor(out=smg[:, 4, :], in0=smg[:, 2, :], in1=smg[:, 3, :], op=ALU.add),
                         reads=[b_sm[g]], writes=[b_sm[g]])
                    t.op("dve", lambda smg=smg: nc.vector.reciprocal(out=smg[:, 5, :], in_=smg[:, 4, :]), reads=[b_sm[g]], writes=[b_sm[g]])
                    pso, b_pso = self.ps_next()
                    for hh in range(4):
                        hd = kvh * 4 + hh
                        q = pis[hh]
                        t.op("dve", lambda q=q, hh=hh, smg=smg: nc.vector.tensor_scalar(out=dg[:, q, :], in0=self.identb[:], scalar1=smg[:, 5, hh:hh + 1],
                                                                                      scalar2=None, op0=ALU.mult),
                             reads=[b_sm[g], self.b_const], writes=[b_dg[q]])
                        pst, b_pst = self.ps_next()
                        for kb in range(3):
                            t.op("pe", lambda pst=pst, q=q, kb=kb: nc.tensor.matmul(pst[:, kb * 128:(kb + 1) * 128], lhsT=p[:, q, kb * 128:(kb + 1) * 128],
                                                                                    rhs=dg[:, q, :], start=True, stop=True),
                                 reads=[b_p[q], b_dg[q]], writes=([b_pst] if kb == 0 else []), signal=(kb == 2))
                        if not t.dry:
                            b_pst.setw(("pe", t.cnt["pe"]))
                        if hh % 2 == 0:
                            t.op("dve", lambda pst=pst, q=q: nc.vector.tensor_copy(out=pT[:, q, :], in_=pst[:, 0:384]), reads=[b_pst], writes=[b_pT[q]])
                        else:
                            t.op("act", lambda pst=pst, q=q: nc.scalar.copy(out=pT[:, q, :], in_=pst[:, 0:384]), reads=[b_pst], writes=[b_pT[q]])
                        for kb in range(3):
                            first = (hh == 0 and kb == 0)
                            t.op("pe", lambda pso=pso, q=q, kb=kb, hh=hh, kvh=kvh, qbk=qbk: nc.tensor.matmul(
                                pso[:, hh * 128:(hh + 1) * 128], lhsT=vt[:, qbk + kb, kvh * 128:(kvh + 1) * 128], rhs=pT[:, q, kb * 128:(kb + 1) * 128],
                                start=(kb == 0), stop=(kb == 2)), reads=[b_vt, b_pT[q]], writes=([b_pso] if first else []),
                                signal=(kb == 2))
                    if not t.dry:
                        b_pso.setw(("pe", t.cnt["pe"]))
                    dsto = oT[:, kvh * 4:(kvh + 1) * 4, qbk * 128:(qbk + 1) * 128]
                    srco = pso[:, :].rearrange("p (a b) -> p a b", a=4)
                    if kvh % 2 == 0:
                        t.op("act", lambda dsto=dsto, srco=srco: nc.scalar.copy(out=dsto, in_=srco), reads=[b_pso], writes=[b_oT])
                    else:
                        t.op("dve", lambda dsto=dsto, srco=srco: nc.vector.tensor_copy(out=dsto, in_=srco), reads=[b_pso], writes=[b_oT])
            for mg in range(4):
                src = wo[:, mg * 512:(mg + 1) * 512].rearrange("(k p) n -> p k n", p=128)
                wb, b_wb = self.wnext([(0, NCH, 512, src)])
                w3 = wb[:, 0:NCH * 512].rearrange("p (k n) -> p k n", k=NCH)
                for mm in range(4):
                    m = mg * 4 + mm
                    ps, b_ps = self.ps_next()
                    for k in range(NCH):
                        t.op("pe", lambda ps=ps, k=k, mm=mm: nc.tensor.matmul(ps[:, 0:n], lhsT=w3[:, k, mm * 128:(mm + 1) * 128], rhs=oT[:, k, 0:n],
                                                                              start=(k == 0), stop=(k == NCH - 1)),
                             reads=[b_wb, b_oT], writes=([b_ps] if k == 0 else []), signal=(k == NCH - 1))
                    if not t.dry:
                        b_ps.setw(("pe", t.cnt["pe"]))
                    if m % 2 == 0:
                        t.op("dve", lambda ps=ps, m=m: nc.vector.tensor_copy(out=ao[:, m, 0:n], in_=ps[:, 0:n]), reads=[b_ps], writes=[b_ao[m]])
                    else:
                        t.op("act", lambda ps=ps, m=m: nc.scalar.copy(out=ao[:, m, 0:n], in_=ps[:, 0:n]), reads=[b_ps], writes=[b_ao[m]])
                self.wrelease()
            self.mix_residual_multi(l, ao, None, b_ao, n, x, b_x, scr, tok0)
        t.barrier()

    def emit_dbg(self):
        t = self.t
        nc = self.nc
        self.phase_init()
        t.op("dve", lambda: nc.vector.memset(self.mod[:], 0.0), writes=[self.b_mod])
        t.op("dve", lambda: nc.vector.memset(self.prm[:], 1.0), writes=[self.b_mod])
        self.phase_t0()
        if self.dbg != "t0":
            self.phase_a1(1, 0, rng=(3, 9))
        if self.dbg == "attn":
            self.phase_a2(1, 0, rng=(4, 7))

    def emit_all(self):
        if self.dbg:
            return self.emit_dbg()
        self.phase_init()
        self.phase_mod()
        self.phase_t0()
        for l in range(self.depth):
            j = l // 2
            if l % 2 == 0:
                self.phase_c1(l, j)
                self.phase_c2(l, j)
            else:
                self.phase_a1(l, j)
                self.phase_a2(l, j)
            self.phase_ffn(l, last=(l == self.depth - 1))

    def build(self):
        self.t.dry = True
        self.emit_all()
        self.t.dry = False
        self.ps_i = 0
        self.nbuf = 0
        self.emit_all()
        self.t.barrier()
        return self.nc


def _core_layout():
    lay = []
    for c in range(4):
        lay.append(("p", 0, c * TOK_CORE))
    for b in range(2):
        for c in range(2):
            lay.append(("s", b, c * TOK_CORE))
    return lay


def _host_inputs(inputs, depth):
    f32 = np.float32
    xp = np.asarray(inputs["x_prompt"], f32)
    xsm = np.asarray(inputs["x_sample"], f32)
    cp = np.asarray(inputs["c_prompt"], f32)
    cs = np.asarray(inputs["c_sample"], f32)
    b_ada = np.asarray(inputs["b_ada"], f32)
    norm_g = np.asarray(inputs["norm_g"], f32)
    wdw = np.asarray(inputs["conv_w_dw"], f32)
    bdw = np.asarray(inputs["conv_b_dw"], f32)
    lng = np.asarray(inputs["conv_ln_g"], f32)
    lnb = np.asarray(inputs["conv_ln_b"], f32)
    sink = np.asarray(inputs["attn_sink"], f32)

    def fm(vec):
        return np.ascontiguousarray(vec.reshape(-1, 128).T)

    ident = np.eye(128, dtype=f32)
    rotT = np.zeros((128, 128), f32)
    for d in range(16):
        rotT[d + 16, d] = -1.0
        rotT[d, d + 16] = 1.0
    cmat = np.stack([ident, rotT])
    inv_freq = (THETA ** (-np.arange(0, ROT, 2, dtype=np.float32) / ROT)).astype(f32)
    shared = {k: np.ascontiguousarray(np.asarray(inputs[k], f32)) for k in
              ("w_ada", "w_gate", "w_up", "w_down", "conv_w_pw1", "conv_w_pw2", "attn_w_q", "attn_w_k", "attn_w_v", "attn_w_o")}
    maps = []
    for (which, b, start) in _core_layout():
        if which == "p":
            xseq, cvec = xp[b], cp[b]
        else:
            xseq, cvec = xsm[b], cs[b]
        slen = xseq.shape[0]
        pos = start - BASEB * 128 + np.arange(NTOK)
        valid = (pos >= 0) & (pos < slen)
        x_ext = np.zeros((NTOK, D), f32)
        x_ext[valid] = xseq[pos[valid]]
        pvec = np.zeros((128, NPV), f32)
        pvec[:, PV_C:PV_C + 16] = fm(cvec)
        for l in range(4):
            o = PV_L + l * PV_LSZ
            pvec[:, o:o + 96] = fm(b_ada[l])
            for g in range(4):
                pvec[:, o + 96 + g * 16:o + 96 + (g + 1) * 16] = fm(norm_g[l, g])
        for jj in range(2):
            o = PV_CONV + jj * PV_CSZ
            for k in range(CW):
                pvec[:, o + k * 16:o + (k + 1) * 16] = fm(wdw[jj, k])
            pvec[:, o + 496:o + 512] = fm(bdw[jj])
            pvec[:, o + 512:o + 528] = fm(lng[jj])
            pvec[:, o + 528:o + 544] = fm(lnb[jj])
            o = PV_ATT + jj * PV_ASZ
            pvec[:, o:o + 16] = sink[jj][None, :]
        tokmask = np.broadcast_to(valid.astype(f32)[None, :], (128, NTOK)).copy()
        ang = pos.astype(f32)[:, None] * inv_freq[None, :]
        cosv = np.cos(ang).astype(f32)
        sinv = np.sin(ang).astype(f32)
        ropec = np.ones((128, NTOK), f32)
        ropes = np.zeros((128, NTOK), f32)
        ropec[0:16] = cosv.T
        ropec[16:32] = cosv.T
        ropes[0:16] = sinv.T
        ropes[16:32] = sinv.T
        amask = np.full((NBLK, 128, 384), NEGM, f32)
        ii = np.arange(128)[:, None]
        jj_ = np.arange(384)[None, :] - 128
        band = np.abs(ii - jj_) <= 128
        for bq in range(NBLK):
            kpos = pos[0] + bq * 128 + jj_[0]
            kv = (kpos >= 0) & (kpos < slen)
            amask[bq][band & kv[None, :]] = 0.0
        m = {"x_ext": x_ext, "pvec": pvec, "tokmask": tokmask, "ropec": ropec, "ropes": ropes, "amask": amask, "cmat": cmat}
        m.update(shared)
        maps.append(m)
    return maps


_PROG_CACHE = {}


def _run(inputs, depth=4):
    if depth not in _PROG_CACHE:
        _PROG_CACHE[depth] = Prog(depth).build()
    nc = _PROG_CACHE[depth]
    maps = _host_inputs(inputs, depth)
    res = run_bass_kernel_spmd(nc, maps, core_ids=list(range(NCORES)))
    outs = [np.asarray(r["y"], np.float32) for r in res.results]
    y_prompt = np.concatenate(outs[0:4], axis=0)[None]
    y_sample = np.stack([np.concatenate(outs[4:6], axis=0), np.concatenate(outs[6:8], axis=0)])
    return y_prompt, y_sample


def kernel(**inputs):
    return _run(inputs, depth=4)
```
